# Optimizing a Trainium2 kernel written in Bass

```python
import jax, jax.numpy as jnp
from jax import lax
import numpy as np

D_MODEL = 2048
BATCH = 16
SEQ = 2048
DEPTH = 2
DEC_BATCH = 16
DEC_SEQ = 64
PAST_LEN = 4096

CHUNK = 64
Q_BLOCK = 128
SB_HEADS = D_MODEL // 256
SB_HEAD_DIM = 128
SB_WIDTH = SB_HEADS * SB_HEAD_DIM
MLA_HEADS = D_MODEL // 256
MLA_NOPE_DIM = 128
MLA_ROPE_DIM = 64
MLA_QK_DIM = MLA_NOPE_DIM + MLA_ROPE_DIM
MLA_V_DIM = 128
MLA_LATENT = D_MODEL // 4
MLA_Q_WIDTH = MLA_HEADS * MLA_QK_DIM
MLA_V_WIDTH = MLA_HEADS * MLA_V_DIM
GATE_WIDTH = 2 * D_MODEL
_END_SB_Q = SB_WIDTH
_END_SB_K = 2 * SB_WIDTH
_END_SB_V = 3 * SB_WIDTH
_END_MLA_Q = _END_SB_V + MLA_Q_WIDTH
_END_CKV = _END_MLA_Q + MLA_LATENT
_END_KROPE = _END_CKV + MLA_ROPE_DIM
IN_SPLITS = (_END_SB_Q, _END_SB_K, _END_SB_V, _END_MLA_Q, _END_CKV, _END_KROPE)
IN_WIDTH = _END_KROPE + GATE_WIDTH
D_FF = 4 * D_MODEL
ROPE_THETA = 10000.0
NORM_EPS = 1e-6
NEG_INF = -1e30

kernel_name = 'stickbreak_mla_gated_hybrid_stream_step'


def rms_norm(x, g):
    xf = x.astype(jnp.float32)
    y = xf * lax.rsqrt(jnp.mean(xf * xf, axis=-1, keepdims=True) + NORM_EPS)
    return (y * g.astype(jnp.float32)).astype(x.dtype)


def apply_rope(x, pos):
    half = x.shape[-1] // 2
    inv_freq = ROPE_THETA ** (-jnp.arange(half, dtype=jnp.float32) / half)
    ang = pos.astype(jnp.float32)[:, None] * inv_freq[None, :]
    cos = jnp.cos(ang)[None, :, None, :]
    sin = jnp.sin(ang)[None, :, None, :]
    xf = x.astype(jnp.float32)
    x1, x2 = xf[..., :half], xf[..., half:]
    return jnp.concatenate([x1 * cos - x2 * sin, x1 * sin + x2 * cos], axis=-1).astype(x.dtype)


def stick_breaking(q, k, v, qpos, kpos):
    scale = q.shape[-1] ** -0.5
    z = jnp.einsum('bqhd,bkhd->bhqk', q, k).astype(jnp.float32) * scale
    mask = kpos[None, :] < qpos[:, None]
    log_1m = jnp.where(mask, jax.nn.log_sigmoid(-z), 0.0)
    log_rest = lax.cumsum(log_1m, axis=3, reverse=True) - log_1m
    a = jnp.where(mask, jnp.exp(jax.nn.log_sigmoid(z) + log_rest), 0.0)
    return jnp.einsum('bhqk,bkhd->bqhd', a.astype(v.dtype), v)


def mla_attend(q, k, v, qpos, kpos):
    scale = MLA_QK_DIM ** -0.5
    s = jnp.einsum('bqhd,bkhd->bhqk', q, k).astype(jnp.float32) * scale
    mask = (kpos // CHUNK)[None, :] <= (qpos // CHUNK)[:, None]
    p = jax.nn.softmax(jnp.where(mask, s, NEG_INF), axis=-1)
    return jnp.einsum('bhqk,bkhd->bqhd', p.astype(v.dtype), v)


def sweep_query_blocks(attend, q):
    B, S, H, D = q.shape
    nb = S // Q_BLOCK
    qb = jnp.moveaxis(q.reshape(B, nb, Q_BLOCK, H, D), 1, 0)
    starts = jnp.arange(nb, dtype=jnp.int32) * Q_BLOCK
    offs = jnp.arange(Q_BLOCK, dtype=jnp.int32)
    out = lax.map(lambda a: attend(a[0], a[1] + offs), (qb, starts))
    return jnp.moveaxis(out, 0, 1).reshape(B, S, H, out.shape[-1])


def mixer_projections(x, pos, norm1_g, w_in, q_norm_g, kv_norm_g):
    B, S, _ = x.shape
    h = rms_norm(x, norm1_g)
    z = jnp.einsum('bsd,dn->bsn', h, w_in)
    sb_q, sb_k, sb_v, q_mla, c_kv, k_rope, gate = jnp.split(z, IN_SPLITS, axis=-1)
    q_mla = q_mla.reshape(B, S, MLA_HEADS, MLA_QK_DIM)
    q_mla = jnp.concatenate([q_mla[..., :MLA_NOPE_DIM], apply_rope(q_mla[..., MLA_NOPE_DIM:], pos)], axis=-1)
    q_mla = rms_norm(q_mla, q_norm_g)
    c_kv = rms_norm(c_kv, kv_norm_g)
    k_rope = apply_rope(k_rope[:, :, None, :], pos)[:, :, 0, :]
    sb_q = sb_q.reshape(B, S, SB_HEADS, SB_HEAD_DIM)
    sb_k = sb_k.reshape(B, S, SB_HEADS, SB_HEAD_DIM)
    sb_v = sb_v.reshape(B, S, SB_HEADS, SB_HEAD_DIM)
    return sb_q, sb_k, sb_v, q_mla, c_kv, k_rope, jax.nn.sigmoid(gate)


def mla_keys_values(c_kv, k_rope, w_uk, w_uv, k_norm_g):
    B, K, _ = c_kv.shape
    k_nope = jnp.einsum('bkl,ln->bkn', c_kv, w_uk).reshape(B, K, MLA_HEADS, MLA_NOPE_DIM)
    v = jnp.einsum('bkl,ln->bkn', c_kv, w_uv).reshape(B, K, MLA_HEADS, MLA_V_DIM)
    kr = jnp.broadcast_to(k_rope[:, :, None, :], (B, K, MLA_HEADS, MLA_ROPE_DIM))
    k = rms_norm(jnp.concatenate([k_nope, kr], axis=-1), k_norm_g)
    return k, v


def merge_and_ffn(x, sb_out, mla_out, gate, w_sb_proj, w_mla_proj, w_o, norm2_g, w_up, w_down):
    B, S, _ = x.shape
    y_sb = sb_out.reshape(B, S, SB_WIDTH) @ w_sb_proj
    y_mla = mla_out.reshape(B, S, MLA_V_WIDTH) @ w_mla_proj
    g_sb, g_mla = jnp.split(gate, 2, axis=-1)
    x = x + (g_sb * y_sb + g_mla * y_mla) @ w_o
    h = rms_norm(x, norm2_g)
    return x + jnp.square(jax.nn.relu(h @ w_up)) @ w_down


def prompt_layer(x, norm1_g, w_in, q_norm_g, k_norm_g, kv_norm_g, w_uk, w_uv,
                 w_sb_proj, w_mla_proj, w_o, norm2_g, w_up, w_down):
    S = x.shape[1]
    pos = jnp.arange(S, dtype=jnp.int32)
    sb_q, sb_k, sb_v, q, c_kv, k_rope, gate = mixer_projections(x, pos, norm1_g, w_in, q_norm_g, kv_norm_g)
    sb_out = sweep_query_blocks(lambda qb, qp: stick_breaking(qb, sb_k, sb_v, qp, pos), sb_q)
    k, v = mla_keys_values(c_kv, k_rope, w_uk, w_uv, k_norm_g)
    mla_out = sweep_query_blocks(lambda qb, qp: mla_attend(qb, k, v, qp, pos), q)
    y = merge_and_ffn(x, sb_out, mla_out, gate, w_sb_proj, w_mla_proj, w_o, norm2_g, w_up, w_down)
    return y, sb_k, sb_v, c_kv, k_rope


def sample_layer(x, past_sb_k, past_sb_v, past_ckv, past_kr, norm1_g, w_in, q_norm_g, k_norm_g,
                 kv_norm_g, w_uk, w_uv, w_sb_proj, w_mla_proj, w_o, norm2_g, w_up, w_down):
    n = x.shape[1]
    past = past_sb_k.shape[1]
    pos = past + jnp.arange(n, dtype=jnp.int32)
    kpos = jnp.arange(past + n, dtype=jnp.int32)
    sb_q, sb_k, sb_v, q, c_kv, k_rope, gate = mixer_projections(x, pos, norm1_g, w_in, q_norm_g, kv_norm_g)
    k_all = jnp.concatenate([past_sb_k.astype(sb_k.dtype), sb_k], axis=1)
    v_all = jnp.concatenate([past_sb_v.astype(sb_v.dtype), sb_v], axis=1)
    sb_out = stick_breaking(sb_q, k_all, v_all, pos, kpos)
    ckv_all = jnp.concatenate([past_ckv.astype(c_kv.dtype), c_kv], axis=1)
    kr_all = jnp.concatenate([past_kr.astype(k_rope.dtype), k_rope], axis=1)
    k, v = mla_keys_values(ckv_all, kr_all, w_uk, w_uv, k_norm_g)
    mla_out = mla_attend(q, k, v, pos, kpos)
    y = merge_and_ffn(x, sb_out, mla_out, gate, w_sb_proj, w_mla_proj, w_o, norm2_g, w_up, w_down)
    return y, sb_k, sb_v, c_kv, k_rope


def setup_inputs(seed: int = 0) -> dict:
    key = jax.random.key(seed)
    ks = jax.random.split(key, 20)
    f32 = jnp.float32
    nrm = lambda k, shape, scale: jax.random.normal(k, shape, f32) * scale
    gain = lambda k, n: 1.0 + 0.02 * jax.random.normal(k, (DEPTH, n), f32)
    return {
        'x_prompt': nrm(ks[0], (BATCH, SEQ, D_MODEL), 1.0),
        'x_sample': nrm(ks[1], (DEC_BATCH, DEC_SEQ, D_MODEL), 1.0),
        'cache_sb_k': nrm(ks[2], (DEPTH, DEC_BATCH, PAST_LEN, SB_HEADS, SB_HEAD_DIM), 1.0),
        'cache_sb_v': nrm(ks[3], (DEPTH, DEC_BATCH, PAST_LEN, SB_HEADS, SB_HEAD_DIM), 1.0),
        'cache_mla_ckv': nrm(ks[4], (DEPTH, DEC_BATCH, PAST_LEN, MLA_LATENT), 1.0),
        'cache_mla_krope': nrm(ks[5], (DEPTH, DEC_BATCH, PAST_LEN, MLA_ROPE_DIM), 1.0),
        'norm1_g': gain(ks[6], D_MODEL),
        'w_in': nrm(ks[7], (DEPTH, D_MODEL, IN_WIDTH), D_MODEL ** -0.5),
        'q_norm_g': gain(ks[8], MLA_QK_DIM),
        'k_norm_g': gain(ks[9], MLA_QK_DIM),
        'kv_norm_g': gain(ks[10], MLA_LATENT),
        'w_uk': nrm(ks[11], (DEPTH, MLA_LATENT, MLA_HEADS * MLA_NOPE_DIM), MLA_LATENT ** -0.5),
        'w_uv': nrm(ks[12], (DEPTH, MLA_LATENT, MLA_V_WIDTH), MLA_LATENT ** -0.5),
        'w_sb_proj': nrm(ks[13], (DEPTH, SB_WIDTH, D_MODEL), SB_WIDTH ** -0.5),
        'w_mla_proj': nrm(ks[14], (DEPTH, MLA_V_WIDTH, D_MODEL), MLA_V_WIDTH ** -0.5),
        'w_o': nrm(ks[15], (DEPTH, D_MODEL, D_MODEL), D_MODEL ** -0.5),
        'norm2_g': gain(ks[16], D_MODEL),
        'w_up': nrm(ks[17], (DEPTH, D_MODEL, D_FF), D_MODEL ** -0.5),
        'w_down': nrm(ks[18], (DEPTH, D_FF, D_MODEL), D_FF ** -0.5),
    }


def reference(x_prompt, x_sample, cache_sb_k, cache_sb_v, cache_mla_ckv, cache_mla_krope,
              norm1_g, w_in, q_norm_g, k_norm_g, kv_norm_g, w_uk, w_uv, w_sb_proj, w_mla_proj,
              w_o, norm2_g, w_up, w_down):
    yp, ys = x_prompt, x_sample
    p_sbk, p_sbv, p_ckv, p_kr = [], [], [], []
    s_sbk, s_sbv, s_ckv, s_kr = [], [], [], []
    for l in range(DEPTH):
        lp = (norm1_g[l], w_in[l], q_norm_g[l], k_norm_g[l], kv_norm_g[l], w_uk[l], w_uv[l],
              w_sb_proj[l], w_mla_proj[l], w_o[l], norm2_g[l], w_up[l], w_down[l])
        yp, a, b, c, d = prompt_layer(yp, *lp)
        p_sbk.append(a); p_sbv.append(b); p_ckv.append(c); p_kr.append(d)
        ys, a, b, c, d = sample_layer(ys, cache_sb_k[l], cache_sb_v[l], cache_mla_ckv[l],
                                      cache_mla_krope[l], *lp)
        s_sbk.append(a); s_sbv.append(b); s_ckv.append(c); s_kr.append(d)
    return (yp, ys, jnp.stack(p_sbk), jnp.stack(p_sbv), jnp.stack(p_ckv), jnp.stack(p_kr),
            jnp.stack(s_sbk), jnp.stack(s_sbv), jnp.stack(s_ckv), jnp.stack(s_kr))
```

```python
import contextlib
import numpy as np
import concourse.bass as bass
import concourse.mybir as mybir
from concourse.bass_utils import run_bass_kernel_spmd

F32 = mybir.dt.float32
BF16 = mybir.dt.bfloat16
AF = mybir.ActivationFunctionType
ALU = mybir.AluOpType

NCORES = 8
D = 2048
NH = 8
HD = 128
LAT = 512
ROPE = 64
QK = 192
INW = 9280
DFF = 8192
PAST = 4096
SEQ = 2048
DSEQ = 64
DEPTH = 2
EPS = 1e-6
C_SBQ, C_SBK, C_SBV, C_MQ, C_CKV, C_KR, C_GSB, C_GMLA = 0, 1024, 2048, 3072, 4608, 5120, 5184, 7232
NSLOT = 17
NSLAB = 3
NDMASEM = 24


def I(method, *args, **kw):
    return lambda e: getattr(e, method)(*args, **kw)


class Region:
    __slots__ = ("name", "last_w", "readers", "cow", "psum")

    def __init__(self, name, psum=False):
        self.name = name
        self.last_w = None
        self.readers = []
        self.cow = []
        self.psum = psum


class Op:
    __slots__ = ("eng", "fn", "deps", "is_dma", "needs_inc", "semval", "dma_id", "idx")


class Sched:
    ENGS = ("pe", "act", "dve", "pool", "sp")

    def __init__(self, nc):
        self.nc = nc
        self.ops = []
        self.dma_ops = {"sp": [], "pool": [], "act": []}
        self.store_ops = []

    def op(self, eng, fn, reads=(), writes=(), dma=False, is_store=False, cowrites=()):
        o = Op()
        o.eng = eng
        o.fn = fn
        o.is_dma = dma
        o.needs_inc = False
        o.semval = None
        o.dma_id = None
        o.idx = len(self.ops)
        deps = {}
        for r in reads:
            if r.last_w is not None:
                deps[r.last_w.idx] = (r.last_w, "raw")
            for cw in r.cow:
                deps[cw.idx] = (cw, "raw")
            if r.psum:
                for rd in r.readers:
                    if rd.eng != eng:
                        deps.setdefault(rd.idx, (rd, "rar"))
        for w in writes:
            if w.last_w is not None:
                deps.setdefault(w.last_w.idx, (w.last_w, "waw"))
            for cw in w.cow:
                deps.setdefault(cw.idx, (cw, "waw"))
            for rd in self._last_readers(w):
                deps.setdefault(rd.idx, (rd, "war"))
        for w in cowrites:
            if w.last_w is not None:
                deps.setdefault(w.last_w.idx, (w.last_w, "waw"))
            for rd in self._last_readers(w):
                deps.setdefault(rd.idx, (rd, "war"))
        if dma:
            lst = self.dma_ops[eng]
            o.dma_id = len(lst)
            if o.dma_id >= NDMASEM:
                prev = lst[o.dma_id - NDMASEM]
                deps.setdefault(prev.idx, (prev, "raw"))
            lst.append(o)
            if is_store:
                self.store_ops.append(o)
        final = []
        for d, kind in deps.values():
            if not d.is_dma and d.eng == eng and not dma:
                if eng == "pe":
                    continue
            if not d.is_dma:
                d.needs_inc = True
            final.append(d)
        o.deps = final
        for r in reads:
            r.readers.append(o)
        for w in writes:
            w.last_w = o
            w.readers = []
            w.cow = []
        for w in cowrites:
            w.cow.append(o)
        self.ops.append(o)
        return o

    @staticmethod
    def _last_readers(w):
        last = {}
        out = []
        for rd in w.readers:
            if rd.is_dma:
                out.append(rd)
            else:
                last[rd.eng] = rd
        return out + list(last.values())

    def finish(self):
        deps = list(self.store_ops)
        last = {}
        for o in self.ops:
            if not o.is_dma:
                last[o.eng] = o
        o = Op()
        o.eng = "sp"
        o.fn = None
        o.is_dma = False
        o.needs_inc = False
        o.semval = None
        o.dma_id = None
        o.idx = len(self.ops)
        for e, lo in last.items():
            if e != "sp":
                lo.needs_inc = True
                deps.append(lo)
        o.deps = deps
        self.ops.append(o)

    def emit(self):
        nc = self.nc
        counters = {e: 0 for e in self.ENGS}
        for o in self.ops:
            if not o.is_dma and o.needs_inc:
                counters[o.eng] += 1
                o.semval = counters[o.eng]
        per_eng = {e: [] for e in self.ENGS}
        for o in self.ops:
            per_eng[o.eng].append(o)
        with contextlib.ExitStack() as st:
            esem = {e: st.enter_context(nc.semaphore("s_" + e)) for e in self.ENGS}
            dsem = {q: [st.enter_context(nc.semaphore("d%s%d" % (q, i))) for i in range(NDMASEM)] for q in ("sp", "pool")}
            block = st.enter_context(nc.Block())

            def run_stream(e, eng):
                seen = {}
                for o in per_eng[e]:
                    for d in o.deps:
                        if d.is_dma:
                            key = ("d", d.eng, d.dma_id % NDMASEM)
                            sem = dsem[d.eng][d.dma_id % NDMASEM]
                            val = 16 * (d.dma_id // NDMASEM + 1)
                        else:
                            key = d.eng
                            sem = esem[d.eng]
                            val = d.semval
                        if seen.get(key, 0) < val:
                            eng.wait_ge(sem, val)
                            seen[key] = val
                    if o.fn is None:
                        continue
                    ins = o.fn(eng)
                    if o.is_dma:
                        ins.then_inc(dsem[o.eng][o.dma_id % NDMASEM], 16)
                    elif o.needs_inc:
                        ins.then_inc(esem[e], 1)

            @block.tensor
            def _(eng):
                run_stream("pe", eng)

            @block.scalar
            def _(eng):
                run_stream("act", eng)

            @block.vector
            def _(eng):
                run_stream("dve", eng)

            @block.gpsimd
            def _(eng):
                run_stream("pool", eng)

            @block.sync
            def _(eng):
                run_stream("sp", eng)


class Seg:
    pass


class Pass:
    pass


class Builder:
    def __init__(self, passes_cfg, n_layers=DEPTH):
        self.n_layers = n_layers
        self.nc = nc = bass.Bass("TRN2", target_bir_lowering=False)
        self.S = Sched(nc)
        self.passes_cfg = passes_cfg
        di = lambda name, shape: nc.dram_tensor(name, list(shape), F32, kind="ExternalInput").ap()
        do = lambda name, shape: nc.dram_tensor(name, list(shape), F32, kind="ExternalOutput").ap()
        self.xp = di("xp", (2, SEQ, D))
        self.xs = di("xs", (2, DSEQ, D))
        self.csk = di("csk", (DEPTH, 2, PAST, 1024))
        self.csv = di("csv", (DEPTH, 2, PAST, 1024))
        self.cckv = di("cckv", (DEPTH, 2, PAST, LAT))
        self.ckr = di("ckr", (DEPTH, 2, PAST, ROPE))
        self.norm1_g = di("norm1_g", (DEPTH, D))
        self.norm2_g = di("norm2_g", (DEPTH, D))
        self.q_norm_g = di("q_norm_g", (DEPTH, QK))
        self.k_norm_g = di("k_norm_g", (DEPTH, QK))
        self.kv_norm_g = di("kv_norm_g", (DEPTH, LAT))
        self.w_in = di("w_in", (DEPTH, D, INW))
        self.w_uk = di("w_uk", (DEPTH, LAT, 1024))
        self.w_uv = di("w_uv", (DEPTH, LAT, 1024))
        self.w_sb_proj = di("w_sb_proj", (DEPTH, 1024, D))
        self.w_mla_proj = di("w_mla_proj", (DEPTH, 1024, D))
        self.w_o = di("w_o", (DEPTH, D, D))
        self.w_up = di("w_up", (DEPTH, D, DFF))
        self.w_down = di("w_down", (DEPTH, DFF, D))
        self.c_mats = di("c_mats", (6, 128, 128))
        self.rope_p = di("rope_p", (SEQ, 128))
        self.rope_s = di("rope_s", (128, 128))
        self.yp = do("yp", (2, SEQ, D))
        self.ys = do("ys", (2, DSEQ, D))
        self.pk = do("pk", (DEPTH, 2, SEQ, 1024))
        self.pv = do("pv", (DEPTH, 2, SEQ, 1024))
        self.pckv = do("pckv", (DEPTH, 2, SEQ, LAT))
        self.pkr = do("pkr", (DEPTH, 2, SEQ, ROPE))
        self.sk = do("sk", (DEPTH, 2, DSEQ, 1024))
        self.sv = do("sv", (DEPTH, 2, DSEQ, 1024))
        self.sckv = do("sckv", (DEPTH, 2, DSEQ, LAT))
        self.skr = do("skr", (DEPTH, 2, DSEQ, ROPE))
        self.hist_regions = {}
        self.Rdram_out = Region("dram_out_misc")
        self.wscr = nc.dram_tensor("wscr", [DEPTH * 120, 128, 4096], BF16, kind="Internal").ap()
        self.wscr_regions = [Region("wscr%d" % i) for i in range(DEPTH * 120)]

    def alloc(self):
        nc = self.nc
        total = 212736
        self.arena = nc.alloc_sbuf_tensor("arena", [128, total // 2], BF16)
        self._off = 0

        def take(nbytes):
            o = self._off
            self._off += (nbytes + 63) // 64 * 64
            assert self._off <= total, ("SBUF overflow", self._off)
            return o
        self.take = take

        def view(off, shape, dt):
            n = 1
            for s in shape[1:]:
                n *= s
            esz = 4 if dt == F32 else 2
            a = self.arena[0:shape[0], off // 2: off // 2 + n * esz // 2]
            if dt == F32:
                a = a.bitcast(F32)
            if len(shape) == 3:
                a = a.rearrange("p (a b) -> p a b", b=shape[2])
            elif len(shape) == 4:
                a = a.rearrange("p (a b c) -> p a b c", b=shape[2], c=shape[3])
            return a
        self.view = view

        def buf(shape, dt, name, off=None):
            n = 1
            for s in shape[1:]:
                n *= s
            nbytes = n * (4 if dt == F32 else 2)
            if off is None:
                off = take(nbytes)
            return view(off, shape, dt), Region(name), off, nbytes

        self.X, self.RX, _, _ = buf([128, 4, D], F32, "X")
        self.HT, self.RHT, _, _ = buf([128, 16, 512], BF16, "HT")
        self.slabs = []
        for i in range(NSLAB):
            v, r, _, _ = buf([128, 8, 512], BF16, "slab%d" % i)
            self.slabs.append((v, r))
        self.CM, self.RCM, _, _ = buf([128, 6, 128], BF16, "cmats")
        self.MSB, self.RMSB, _, _ = buf([128, 128], F32, "mask_sb_f32")
        self.G1T, self.RG, _, _ = buf([128, DEPTH, 16], F32, "g1T")
        self.G2T, _, _, _ = buf([128, DEPTH, 16], F32, "g2T")
        self.GKV, _, _, _ = buf([128, DEPTH, LAT], F32, "gkv")
        self.GQK, _, _, _ = buf([128, DEPTH, 4], F32, "gqk")
        self.CQ, self.RCQ, _, _ = buf([128, DEPTH, 2], F32, "cq")
        self.ROPE, self.RROPE, _, _ = buf([128, 4, 128], F32, "rope")
        self.ST, self.RST, _, _ = buf([128, 64], F32, "stats")
        self.ST2, self.RST2, _, _ = buf([128, 64], F32, "stats2")
        self.SBO, self.RSBO, _, _ = buf([128, 8, 512], BF16, "sb_outT")
        self.MLO, self.RMLO, _, _ = buf([128, 8, 512], BF16, "mla_outT")
        self.E = [buf([128, 512], F32, "E%d" % i)[:2] for i in range(5)]
        self.L = [buf([128, 512], BF16, "L%d" % i)[:2] for i in range(3)]
        self.A = [buf([128, 512], BF16, "A%d" % i)[:2] for i in range(3)]
        self.LS = [[buf([128, 512], BF16, "LS%d_%d" % (i, k))[:2] for k in range(3)] for i in range(2)]
        self.ls_rot = [0, 0]
        self.c_rot = 0
        self.P = self.L
        self.SQ = self.A
        self.R = self.E
        self.ZOUT = [buf([128, 512], F32, "ZOUT%d" % i)[:2] for i in range(2)]
        self.QB = [buf([128, 2, QK], BF16, "QB%d" % i)[:2] for i in range(2)]
        self.RT = [buf([128, 2, 64], F32, "RT%d" % i)[:2] for i in range(2)]
        self.RT2 = [buf([128, 2, 64], F32, "RTb%d" % i)[:2] for i in range(2)]
        self.XN, self.RXN, o_xn, _ = buf([128, D], BF16, "XN")
        self.G = [(view(o_xn + i * 2048, [128, 512], F32), Region("G%d" % i)) for i in range(2)]
        o_ls = take(9216)
        self.KTOK = [(view(o_ls + i * 2048, [128, 4, 256], BF16), Region("ktok%d" % i)) for i in range(2)]
        self.CTOK = [(view(o_ls + i * 4096, [128, 4, 512], BF16), Region("ctok%d" % i)) for i in range(2)]
        self.KRTOK = [(view(o_ls + 8192 + i * 512, [128, 4, 64], BF16), Region("krtok%d" % i)) for i in range(2)]
        self.R_LS = Region("loadstage")
        o_ar = take(29184)
        self.R_AR = Region("arena_phase")
        self.QT = view(o_ar, [128, 8, 512], BF16); self.RQT = Region("qT")
        self.SBK = view(o_ar + 8192, [128, 4, 1024], BF16); self.RSBK = Region("sbk")
        self.SBV = view(o_ar + 16384, [128, 4, 1024], BF16); self.RSBV = Region("sbv")
        self.QNT = view(o_ar, [128, 8, 512], BF16); self.RQNT = Region("qnT")
        self.QRT = view(o_ar + 8192, [128, 8, 512], BF16); self.RQRT = Region("qrT")
        self.CKVB = view(o_ar + 16384, [128, 4, 512], BF16); self.RCKVB = Region("ckvb")
        self.KRB = view(o_ar + 20480, [128, 4, 64], BF16); self.RKRB = Region("krb")
        self.WUK = [(view(o_ar + 20992 + i * 2048, [128, 4, 256], BF16), Region("wuk%d" % i)) for i in range(2)]
        self.WUV = [(view(o_ar + 25088 + i * 2048, [128, 4, 256], BF16), Region("wuv%d" % i)) for i in range(2)]
        self.UT = view(o_ar, [128, 16, 512], BF16); self.RUT = Region("uT")
        self.arena_regions = [self.RQT, self.RSBK, self.RSBV, self.RQNT, self.RQRT, self.RCKVB, self.RKRB,
                              self.WUK[0][1], self.WUK[1][1], self.WUV[0][1], self.WUV[1][1], self.RUT,
]
        o_st = take(40448)
        NK = NSLOT * 128
        self.KST = view(o_st, [128, 2, NK], BF16); self.RKST = Region("kst")
        self.VST = view(o_st + 8704, [128, NSLOT, 256], BF16); self.RVST = Region("vst")
        self.CKVT = view(o_st + 17408, [128, 4, NK], BF16); self.RCKVT = Region("ckvT")
        self.KRT = view(o_st + 34816, [128, NK], BF16); self.RKRT = Region("krT")
        self.RSTDK = view(o_st + 39168, [128, 2, NSLOT], F32); self.RRSTDK = Region("rstdk")
        self.SSR = view(o_st + 39168 + 192, [128, NSLOT], F32); self.RSSR = Region("ssr")
        self.AT = [(view(o_st + i * 16384, [128, 16, 512], BF16), Region("aT%d" % i)) for i in range(2)]
        self.TMPA = view(o_st, [128, 4, 512], F32); self.RTMPA = Region("tmpA")
        self.TMPB = view(o_st + 8192, [128, 4, 512], F32); self.RTMPB = Region("tmpB")
        self.SG = [(view(o_st + 16384 + i * 2048, [128, 512], F32), Region("sg%d" % i)) for i in range(2)]
        self.staging_regions = [self.RKST, self.RVST, self.RCKVT, self.RKRT, self.RRSTDK, self.RSSR,
                                self.AT[0][1], self.AT[1][1], self.RTMPA, self.RTMPB, self.SG[0][1], self.SG[1][1]]
        self.ls_regions = [r for (_, r) in self.KTOK + self.CTOK + self.KRTOK]
        self.banks = []
        for i in range(8):
            t = nc.alloc_psum_tensor("bank%d" % i, [128, 512], F32)
            self.banks.append((t, Region("bank%d" % i, psum=True)))
        self.TV = [(self.banks[6][0][:, :].bitcast(BF16)[:, 0:512], self.banks[6][1]),
                   (self.banks[7][0][:, :].bitcast(BF16)[:, 0:512], self.banks[7][1])]
        self.STB = (self.banks[7][0][:, 256:512], self.banks[7][1])
        self.gemm_rot = 0
        self.z_rot = 0
        self.t_rot = 0
        self.evac_rot = 0
        self.work_rot = {}

    @staticmethod
    def pipeline(stages, n):
        depth = max(off for off, _ in stages)
        for k in range(n + depth):
            for off, fn in stages:
                if 0 <= k - off < n:
                    fn(k - off)

    def rot(self, lst, key):
        i = self.work_rot.get(key, 0)
        self.work_rot[key] = i + 1
        return lst[i % len(lst)]

    def phase_switch(self, regions_new, regions_old):
        S = self.S
        S.op("dve", I("memset", self.ST2[0:1, 63:64], 0.0), reads=[], writes=list(regions_new) + list(regions_old) + [self.RST2])

    def bank(self, i):
        return self.banks[i]

    def next_gemm_banks(self, n):
        res = []
        for _ in range(n):
            res.append(self.banks[self.gemm_rot % 6])
            self.gemm_rot += 1
        return res

    def next_tv(self):
        self.t_rot += 1
        return self.TV[self.t_rot % 2]

    def evac_engine(self):
        self.evac_rot += 1
        return "act" if self.evac_rot % 2 == 0 else "dve"

    def copy_op(self, eng, out, in_, reads, writes, scale=None):
        S = self.S
        if eng == "act":
            if scale is None:
                S.op("act", I("activation", out=out, in_=in_, func=AF.Copy), reads=reads, writes=writes)
            else:
                S.op("act", I("activation", out=out, in_=in_, func=AF.Copy, scale=scale), reads=reads, writes=writes)
        else:
            if scale is None:
                S.op("dve", I("tensor_copy", out=out, in_=in_), reads=reads, writes=writes)
            else:
                S.op("dve", I("tensor_scalar", out=out, in0=in_, scalar1=scale, scalar2=None, op0=ALU.mult), reads=reads, writes=writes)

    def rstd_op(self, out, ss, n, reads, writes, tmp):
        S = self.S
        S.op("dve", I("tensor_scalar", out=tmp, in0=ss, scalar1=1.0 / n, scalar2=EPS, op0=ALU.mult, op1=ALU.add),
             reads=reads, writes=writes)
        S.op("act", I("activation", out=tmp, in_=tmp, func=AF.Ln), reads=writes, writes=writes)
        S.op("act", I("activation", out=out, in_=tmp, func=AF.Exp, scale=-0.5), reads=writes, writes=writes)

    def slab_plan_layer(self, l):
        plan = []
        w_in = self.w_in

        def add(W2d, r0, nk, c0, nc_):
            plan.append((W2d, r0, nk, c0, nc_))
        for cg in range(6):
            for kh in range(2):
                add(w_in[l], kh * 1024, 8, cg * 512, 512)
        for cg in range(4):
            for kh in range(2):
                add(w_in[l], kh * 1024, 8, C_MQ + cg * 384, 384)
        for kh in range(2):
            add(w_in[l], kh * 1024, 8, C_CKV, 512)
        for kh in range(2):
            add(w_in[l], kh * 1024, 8, C_KR, 64)
        for cg in range(4):
            add(self.w_mla_proj[l], 0, 8, cg * 512, 512)
            for kh in range(2):
                add(w_in[l], kh * 1024, 8, C_GMLA + cg * 512, 512)
            add(self.w_sb_proj[l], 0, 8, cg * 512, 512)
            for kh in range(2):
                add(w_in[l], kh * 1024, 8, C_GSB + cg * 512, 512)
        for cg in range(4):
            for kh in range(2):
                add(self.w_o[l], kh * 1024, 8, cg * 512, 512)
        for q in range(4):
            for cg in range(4):
                for kh in range(2):
                    add(self.w_up[l], kh * 1024, 8, q * 2048 + cg * 512, 512)
            for cg in range(4):
                for kh in range(2):
                    add(self.w_down[l], q * 2048 + kh * 1024, 8, cg * 512, 512)
        return plan

    def slab_init(self, n_passes):
        self.slab_list = []
        for p in range(n_passes):
            for l in range(self.n_layers):
                plan = self.slab_plan_layer(l)
                assert len(plan) == 120
                for j, spec in enumerate(plan):
                    self.slab_list.append(spec + (l * 120 + j, p == 0))
        self.slab_issued = 0
        self.slab_used = 0

    def slab_issue(self):
        i = self.slab_issued
        if i >= len(self.slab_list):
            return
        W2d, r0, nk, c0, nc_, sid, first = self.slab_list[i]
        buf, reg = self.slabs[i % NSLAB]
        dst = buf[:, 0:nk, 0:nc_]
        scr = self.wscr[sid].rearrange("p (k c) -> p k c", c=512)[:, 0:nk, 0:nc_]
        if first:
            src = W2d[r0:r0 + nk * 128, c0:c0 + nc_].rearrange("(k p) c -> p k c", p=128)
            self.S.op("pool", I("dma_start", out=dst, in_=src), writes=[reg], dma=True)
            self.S.op("sp", I("dma_start", out=scr, in_=dst), reads=[reg], writes=[self.wscr_regions[sid]], dma=True)
        else:
            self.S.op("sp", I("dma_start", out=dst, in_=scr), reads=[self.wscr_regions[sid]], writes=[reg], dma=True)
        self.slab_issued += 1

    def slab_next(self, expect):
        i = self.slab_used
        spec = self.slab_list[i]
        assert (spec[1], spec[2], spec[3], spec[4]) == expect[1:], (spec[1:], expect[1:])
        while self.slab_issued < min(i + NSLAB, len(self.slab_list)):
            self.slab_issue()
        self.slab_used += 1
        return self.slabs[i % NSLAB]

    def gemm(self, orient, act, ract, W2d, r0, nkh, c0, ncols, T, nsub, evac):
        S = self.S
        nb = nsub if orient == "tok" else (ncols + 127) // 128
        banks = self.next_gemm_banks(nb)
        for kh in range(nkh):
            slab, rslab = self.slab_next((W2d, r0 + kh * 1024, 8, c0, ncols))
            for b in range(nb):
                bt, br = banks[b]
                for kl in range(8):
                    kc = kh * 8 + kl
                    first = (kh == 0 and kl == 0)
                    last = (kh == nkh - 1 and kl == 7)
                    if orient == "tok":
                        out = bt[:, 0:ncols]
                        lhsT = act(kc)[:, b * 128:(b + 1) * 128]
                        rhs = slab[:, kl, 0:ncols]
                    else:
                        w = min(128, ncols - b * 128)
                        out = bt[0:w, 0:T]
                        lhsT = slab[:, kl, b * 128:b * 128 + w]
                        rhs = act(kc)[:, 0:T]
                    S.op("pe", I("matmul", out, lhsT=lhsT, rhs=rhs, start=first, stop=last),
                         reads=list(ract) + [rslab], writes=[br])
        for b in range(nb):
            bt, br = banks[b]
            evac(b, bt, br)

    def norm_to_hT(self, ps, GT, l):
        S = self.S
        ident = self.CM[:, 0, :]
        for sub in range(ps.nsub):
            xs_ = self.X[:, sub, :]
            ss = self.ST[:, sub:sub + 1]
            S.op("act", I("activation", out=self.XN[:, :], in_=xs_, func=AF.Square, accum_out=ss),
                 reads=[self.RX], writes=[self.RXN, self.RST])
            rs = self.ST[:, 8 + sub:9 + sub]
            tmp = self.ST[:, 16 + sub:17 + sub]
            self.rstd_op(rs, ss, D, [self.RST], [self.RST], tmp)
            S.op("dve", I("tensor_scalar", out=self.XN[:, :], in0=xs_, scalar1=rs, scalar2=None, op0=ALU.mult),
                 reads=[self.RX, self.RST], writes=[self.RXN])
            for g4 in range(4):
                tv, tr_ = self.next_tv()
                for k4 in range(4):
                    kc = g4 * 4 + k4
                    S.op("pe", I("transpose", out=tv[:, k4 * 128:(k4 + 1) * 128], in_=self.XN[:, kc * 128:(kc + 1) * 128], identity=ident),
                         reads=[self.RXN, self.RCM], writes=[tr_])
                for k4 in range(4):
                    kc = g4 * 4 + k4
                    self.copy_op(self.evac_engine(), self.HT[:, kc, sub * 128:(sub + 1) * 128], tv[:, k4 * 128:(k4 + 1) * 128],
                                 [tr_, self.RG], [self.RHT], scale=GT[:, l, kc:kc + 1])

    def sb_proj(self, ps, l):
        S = self.S
        T, nsub = ps.T, ps.nsub
        act = lambda kc: self.HT[:, kc, 0:T]
        for cg in range(2):
            def evac(j, bt, br, cg=cg):
                h = cg * 4 + j
                self.copy_op(self.evac_engine(), self.QT[:, h, 0:T], bt[:, 0:T], [br], [self.RQT], scale=float(HD) ** -0.5)
            self.gemm("feat", act, [self.RHT], self.w_in[l], 0, 2, C_SBQ + cg * 512, 512, T, nsub, evac)
        for name, c_base, dst_b, rdst in (("k", C_SBK, self.SBK, self.RSBK), ("v", C_SBV, self.SBV, self.RSBV)):
            for cg in range(2):
                def evac(sub, bt, br, cg=cg, name=name, dst_b=dst_b, rdst=rdst):
                    zo, rzo = self.rot(self.ZOUT, "zout")
                    S.op("act", I("activation", out=zo[:, :], in_=bt[:, :], func=AF.Copy), reads=[br], writes=[rzo])
                    S.op("dve", I("tensor_copy", out=dst_b[:, sub, cg * 512:(cg + 1) * 512], in_=zo[:, :]), reads=[rzo], writes=[rdst])
                    ps.store_kv(self, l, name, sub, zo, rzo, cg * 512, 512)
                self.gemm("tok", act, [self.RHT], self.w_in[l], 0, 2, c_base + cg * 512, 512, T, nsub, evac)

    def key_superchunks(self, seg):
        n_own = len(seg.subs)
        tiles = []
        for j in range(n_own - 1, -1, -1):
            c0 = j * 128
            ncol = seg.nq - c0
            tiles.append(dict(own=True, j=j, r=seg.rows, c0=c0, dc=min(128, ncol)))
        nch = seg.hist_n // 512
        chunks = list(range(nch - 1, -1, -1))
        scs = []
        cur = dict(own=tiles, hist=[], order=list(tiles))
        slot = 0
        for t in tiles:
            t["slot"] = slot
            slot += 1
        for c in chunks:
            if slot + 4 > NSLOT:
                scs.append(cur)
                cur = dict(own=[], hist=[], order=[])
                slot = 0
            ch = dict(c=c, slot0=slot, tiles=[])
            for k in range(3, -1, -1):
                t = dict(own=False, kt=c * 4 + k, r=128, c0=0, dc=0, slot=slot + k)
                ch["tiles"].append(t)
                cur["order"].append(t)
            cur["hist"].append(ch)
            slot += 4
        scs.append(cur)
        return scs

    def sb_attention(self, ps, l):
        S = self.S
        ident = self.CM[:, 0, :]
        triGE = self.CM[:, 1, :]
        triLT = self.CM[:, 2, :]
        for seg in ps.segs:
            scs = self.key_superchunks(seg)
            nq = seg.nq
            for hp in range(4):
                state = [self.banks[0], self.banks[1]]
                lsum = [None, None]
                ones = self.CM[:, 5, :]
                first_tile = [True, True]
                n_tiles_total = sum(len(sc["order"]) for sc in scs)
                done = 0
                for sc in scs:
                    for t in sc["own"]:
                        sub = seg.subs[t["j"]]
                        tv, tr_ = self.next_tv()
                        for i in range(2):
                            h = hp * 2 + i
                            S.op("pe", I("transpose", out=tv[:, i * 128:(i + 1) * 128], in_=self.SBK[:, sub, h * 128:(h + 1) * 128], identity=ident),
                                 reads=[self.RSBK, self.RCM], writes=[tr_])
                        for i in range(2):
                            self.copy_op(self.evac_engine(), self.KST[:, i, t["slot"] * 128:(t["slot"] + 1) * 128], tv[:, i * 128:(i + 1) * 128],
                                         [tr_], [self.RKST])
                    for ch in sc["hist"]:
                        c = ch["c"]
                        ktok, rktok = self.rot(self.KTOK, "ktok")
                        srck, rk = seg.hist_src(self, l, "k", c)
                        srcv, rv = seg.hist_src(self, l, "v", c)
                        S.op("pool", I("dma_start", out=ktok[:, :, :], in_=srck[:, hp * 256:(hp + 1) * 256].rearrange("(k p) c -> p k c", p=128)),
                             reads=[rk], writes=[rktok], dma=True)
                        s0 = ch["slot0"]
                        S.op("pool", I("dma_start", out=self.VST[:, s0:s0 + 4, :], in_=srcv[:, hp * 256:(hp + 1) * 256].rearrange("(k p) c -> p k c", p=128)),
                             reads=[rv], writes=[self.RVST], dma=True)
                        for i in range(2):
                            tv, tr_ = self.next_tv()
                            for k in range(4):
                                S.op("pe", I("transpose", out=tv[:, k * 128:(k + 1) * 128], in_=ktok[:, k, i * 128:(i + 1) * 128], identity=ident),
                                     reads=[rktok, self.RCM], writes=[tr_])
                            self.copy_op(self.evac_engine(), self.KST[:, i, s0 * 128:(s0 + 4) * 128], tv[:, :], [tr_], [self.RKST])
                    units = []
                    for t in sc["order"]:
                        done += 1
                        for i in range(2):
                            u = dict(t=t, i=i, h=hp * 2 + i, r=t["r"], c0=t["c0"], ncol=nq - t["c0"],
                                     ft=first_tile[i], last=(done == n_tiles_total))
                            first_tile[i] = False
                            units.append(u)

                    def s0(k):
                        u = units[k]
                        r, c0, ncol = u["r"], u["c0"], u["ncol"]
                        u["z"] = self.banks[4 + self.z_rot % 2]
                        self.z_rot += 1
                        zb, rz = u["z"]
                        ksl = self.KST[:, u["i"], u["t"]["slot"] * 128:u["t"]["slot"] * 128 + r]
                        qsl = self.QT[:, u["h"], seg.qc0 + c0:seg.qc0 + nq]
                        S.op("pe", I("matmul", zb[0:r, 0:ncol], lhsT=ksl, rhs=qsl, start=True, stop=True),
                             reads=[self.RKST, self.RQT], writes=[rz])

                    def s1(k):
                        u = units[k]
                        r, ncol = u["r"], u["ncol"]
                        zb, rz = u["z"]
                        u["E"] = self.rot(self.E, "E")
                        Eb, rE = u["E"]
                        S.op("act", I("activation", out=Eb[0:r, 0:ncol], in_=zb[0:r, 0:ncol], func=AF.Exp), reads=[rz], writes=[rE])
                        if u["t"]["own"]:
                            dc = u["t"]["dc"]
                            S.op("dve", I("tensor_tensor", out=Eb[0:r, 0:dc], in0=Eb[0:r, 0:dc], in1=self.MSB[0:r, 0:dc], op=ALU.mult),
                                 reads=[rE, self.RMSB], writes=[rE])

                    def s2(k):
                        u = units[k]
                        r, ncol = u["r"], u["ncol"]
                        Eb, rE = u["E"]
                        u["L"] = self.rot(self.L, "L")
                        Lb, rL = u["L"]
                        S.op("act", I("activation", out=Lb[0:r, 0:ncol], in_=Eb[0:r, 0:ncol], func=AF.Ln, bias=1.0), reads=[rE], writes=[rL])

                    def s3(k):
                        u = units[k]
                        i, r, c0, ncol, ft = u["i"], u["r"], u["c0"], u["ncol"], u["ft"]
                        Lb, rL = u["L"]
                        u["C"] = self.banks[2 + self.c_rot % 2]
                        self.c_rot += 1
                        cb, rc = u["C"]
                        if not ft:
                            Lp, rLp = lsum[i]
                            S.op("pe", I("matmul", cb[:, 0:ncol], lhsT=ones[:, :], rhs=Lp[:, c0:c0 + ncol], start=True, stop=False),
                                 reads=[rLp, self.RCM], writes=[rc])
                        S.op("pe", I("matmul", cb[:, 0:ncol], lhsT=triGE[0:r, :], rhs=Lb[0:r, 0:ncol], start=ft, stop=True),
                             reads=[rL, self.RCM], writes=[rc])
                        if not u["last"]:
                            self.ls_rot[i] += 1
                            Ln_, rLn = self.LS[i][self.ls_rot[i] % 3]
                            if (c0 > 0) or (r < 128):
                                S.op("pool", I("memset", Ln_[:, 0:nq], 0.0), writes=[rLn])
                            if ft:
                                S.op("dve", I("tensor_copy", out=Ln_[0:r, c0:nq], in_=Lb[0:r, 0:ncol]), reads=[rL], writes=[rLn])
                            else:
                                Lp, rLp = lsum[i]
                                S.op("dve", I("tensor_tensor", out=Ln_[0:r, c0:nq], in0=Lp[0:r, c0:nq], in1=Lb[0:r, 0:ncol], op=ALU.add),
                                     reads=[rLp, rL], writes=[rLn])
                            lsum[i] = (Ln_, rLn)

                    def s4(k):
                        u = units[k]
                        r, ncol = u["r"], u["ncol"]
                        cb, rc = u["C"]
                        u["G"] = self.rot(self.G, "G")
                        Gb, rG = u["G"]
                        S.op("act", I("activation", out=Gb[0:r, 0:ncol], in_=cb[0:r, 0:ncol], func=AF.Exp, scale=-1.0), reads=[rc], writes=[rG])

                    def s5(k):
                        u = units[k]
                        r, ncol = u["r"], u["ncol"]
                        Eb, rE = u["E"]
                        Gb, rG = u["G"]
                        u["A"] = self.rot(self.A, "A")
                        Ab, rA = u["A"]
                        S.op("dve", I("tensor_tensor", out=Ab[0:r, 0:ncol], in0=Eb[0:r, 0:ncol], in1=Gb[0:r, 0:ncol], op=ALU.mult),
                             reads=[rE, rG], writes=[rA])

                    def s6(k):
                        u = units[k]
                        i, h, r, c0, ncol, t = u["i"], u["h"], u["r"], u["c0"], u["ncol"], u["t"]
                        Ab, rA = u["A"]
                        avb, rav = state[i]
                        if t["own"]:
                            sub = seg.subs[t["j"]]
                            vsl = self.SBV[0:r, sub, h * 128:(h + 1) * 128]
                            rvs = self.RSBV
                        else:
                            vsl = self.VST[0:r, t["slot"], i * 128:(i + 1) * 128]
                            rvs = self.RVST
                        S.op("pe", I("matmul", avb[:, c0:c0 + ncol], lhsT=vsl, rhs=Ab[0:r, 0:ncol], start=u["ft"], stop=u["last"], skip_group_check=True),
                             reads=[rvs, rA], writes=[rav])
                    self.pipeline([(0, s0), (1, s1), (2, s2), (3, s3), (4, s4), (5, s5), (6, s6)], len(units))
                for i in range(2):
                    h = hp * 2 + i
                    avb, rav = state[i]
                    self.copy_op(self.evac_engine(), self.SBO[:, h, seg.qc0:seg.qc0 + nq], avb[:, 0:nq], [rav], [self.RSBO])

    def mla_proj(self, ps, l):
        S = self.S
        T, nsub = ps.T, ps.nsub
        ident = self.CM[:, 0, :]
        act = lambda kc: self.HT[:, kc, 0:T]
        for cg in range(4):
            def evac(sub, bt, br, cg=cg):
                pv = bt[:, 0:384].rearrange("p (h c) -> p h c", c=QK)
                ssq = self.ST[:, 24:26]
                for hh in range(2):
                    S.op("act", I("activation", out=self.XN[:, hh * QK:(hh + 1) * QK], in_=bt[:, hh * QK:(hh + 1) * QK], func=AF.Square, accum_out=self.ST[:, 24 + hh:25 + hh]),
                         reads=[br], writes=[self.RXN, self.RST])
                rs = self.ST[:, 26:28]
                self.rstd_op(rs, ssq, QK, [self.RST], [self.RST], self.ST[:, 28:30])
                qb, rqb = self.rot(self.QB, "qb")
                rt, rrt = self.rot(self.RT, "rt")
                rt2, rrt2 = self.rot(self.RT2, "rt2")
                cosr = self.ROPE[:, sub, 0:64].rearrange("p (h c) -> p h c", c=32)
                sinr = self.ROPE[:, sub, 64:128].rearrange("p (h c) -> p h c", c=32)
                x1 = pv[:, :, 128:160]
                x2 = pv[:, :, 160:192]
                S.op("dve", I("tensor_tensor", out=rt[:, :, 0:32], in0=x1, in1=cosr, op=ALU.mult), reads=[br, self.RROPE], writes=[rrt])
                S.op("dve", I("tensor_tensor", out=rt2[:, :, 0:32], in0=x2, in1=sinr, op=ALU.mult), reads=[br, self.RROPE], writes=[rrt2])
                S.op("dve", I("tensor_tensor", out=rt[:, :, 32:64], in0=x1, in1=sinr, op=ALU.mult), reads=[br, self.RROPE], writes=[rrt])
                S.op("dve", I("tensor_tensor", out=rt2[:, :, 32:64], in0=x2, in1=cosr, op=ALU.mult), reads=[br, self.RROPE], writes=[rrt2])
                S.op("dve", I("tensor_tensor", out=rt[:, :, 0:32], in0=rt[:, :, 0:32], in1=rt2[:, :, 0:32], op=ALU.subtract), reads=[rrt, rrt2], writes=[rrt])
                S.op("dve", I("tensor_tensor", out=rt[:, :, 32:64], in0=rt[:, :, 32:64], in1=rt2[:, :, 32:64], op=ALU.add), reads=[rrt, rrt2], writes=[rrt])
                for hh in range(2):
                    S.op("dve", I("tensor_scalar", out=qb[:, hh, 0:128], in0=bt[:, hh * QK:hh * QK + 128], scalar1=self.ST[:, 26 + hh:27 + hh], scalar2=None, op0=ALU.mult),
                         reads=[br, self.RST], writes=[rqb])
                    S.op("dve", I("tensor_scalar", out=qb[:, hh, 128:192], in0=rt[:, hh, :], scalar1=self.ST[:, 26 + hh:27 + hh], scalar2=None, op0=ALU.mult),
                         reads=[rrt, self.RST], writes=[rqb])
                tv, tr_ = self.next_tv()
                for hh in range(2):
                    S.op("pe", I("transpose", out=tv[:, hh * 128:(hh + 1) * 128], in_=qb[:, hh, 0:128], identity=ident),
                         reads=[rqb, self.RCM], writes=[tr_])
                    S.op("pe", I("transpose", out=tv[0:64, 256 + hh * 128:256 + (hh + 1) * 128], in_=qb[:, hh, 128:192], identity=ident),
                         reads=[rqb, self.RCM], writes=[tr_])
                for hh in range(2):
                    h = cg * 2 + hh
                    self.copy_op(self.evac_engine(), self.QNT[:, h, sub * 128:(sub + 1) * 128], tv[:, hh * 128:(hh + 1) * 128],
                                 [tr_, self.RCQ], [self.RQNT], scale=self.CQ[:, l, 0:1])
                    self.copy_op(self.evac_engine(), self.QRT[0:64, h, sub * 128:(sub + 1) * 128], tv[0:64, 256 + hh * 128:256 + (hh + 1) * 128],
                                 [tr_, self.RCQ], [self.RQRT], scale=self.CQ[0:64, l, 1:2])
            self.gemm("tok", act, [self.RHT], self.w_in[l], 0, 2, C_MQ + cg * 384, 384, T, nsub, evac)

        def evac_ckv(sub, bt, br):
            S.op("act", I("activation", out=self.XN[:, 0:LAT], in_=bt[:, :], func=AF.Square, accum_out=self.ST[:, 32:33]),
                 reads=[br], writes=[self.RXN, self.RST])
            self.rstd_op(self.ST[:, 33:34], self.ST[:, 32:33], LAT, [self.RST], [self.RST], self.ST[:, 34:35])
            zo, rzo = self.rot(self.ZOUT, "zout")
            S.op("dve", I("scalar_tensor_tensor", out=zo[:, :], in0=bt[:, :], scalar=self.ST[:, 33:34], in1=self.GKV[:, l, :], op0=ALU.mult, op1=ALU.mult),
                 reads=[br, self.RST, self.RG], writes=[rzo])
            S.op("act", I("activation", out=self.CKVB[:, sub, :], in_=zo[:, :], func=AF.Copy), reads=[rzo], writes=[self.RCKVB])
            ps.store_kv(self, l, "ckv", sub, zo, rzo, 0, LAT)
        self.gemm("tok", act, [self.RHT], self.w_in[l], 0, 2, C_CKV, 512, T, nsub, evac_ckv)

        def evac_kr(sub, bt, br):
            zo, rzo = self.rot(self.ZOUT, "zout")
            rt, rrt = self.rot(self.RT, "rt")
            rt2, rrt2 = self.rot(self.RT2, "rt2")
            cos = self.ROPE[:, sub, 0:32]
            sin = self.ROPE[:, sub, 64:96]
            x1 = bt[:, 0:32]
            x2 = bt[:, 32:64]
            S.op("dve", I("tensor_tensor", out=rt[:, 0, 0:32], in0=x1, in1=cos, op=ALU.mult), reads=[br, self.RROPE], writes=[rrt])
            S.op("dve", I("tensor_tensor", out=rt2[:, 0, 0:32], in0=x2, in1=sin, op=ALU.mult), reads=[br, self.RROPE], writes=[rrt2])
            S.op("dve", I("tensor_tensor", out=rt[:, 0, 32:64], in0=x1, in1=sin, op=ALU.mult), reads=[br, self.RROPE], writes=[rrt])
            S.op("dve", I("tensor_tensor", out=rt2[:, 0, 32:64], in0=x2, in1=cos, op=ALU.mult), reads=[br, self.RROPE], writes=[rrt2])
            S.op("dve", I("tensor_tensor", out=zo[:, 0:32], in0=rt[:, 0, 0:32], in1=rt2[:, 0, 0:32], op=ALU.subtract), reads=[rrt, rrt2], writes=[rzo])
            S.op("dve", I("tensor_tensor", out=zo[:, 32:64], in0=rt[:, 0, 32:64], in1=rt2[:, 0, 32:64], op=ALU.add), reads=[rrt, rrt2], writes=[rzo])
            S.op("act", I("activation", out=self.KRB[:, sub, :], in_=zo[:, 0:64], func=AF.Copy), reads=[rzo], writes=[self.RKRB])
            ps.store_kv(self, l, "kr", sub, zo, rzo, 0, ROPE)
        self.gemm("tok", act, [self.RHT], self.w_in[l], 0, 2, C_KR, 64, T, nsub, evac_kr)

    def mla_attention(self, ps, l):
        S = self.S
        ident = self.CM[:, 0, :]
        ones = self.CM[:, 5, :]
        for seg in ps.segs:
            scs = self.key_superchunks(seg)
            nq = seg.nq
            for hp in range(4):
                wuk, rwuk = self.rot(self.WUK, "wuk")
                wuv, rwuv = self.rot(self.WUV, "wuv")
                S.op("pool", I("dma_start", out=wuk[:, :, :], in_=self.w_uk[l][:, hp * 256:(hp + 1) * 256].rearrange("(k p) c -> p k c", p=128)),
                     writes=[rwuk], dma=True)
                S.op("pool", I("dma_start", out=wuv[:, :, :], in_=self.w_uv[l][:, hp * 256:(hp + 1) * 256].rearrange("(k p) c -> p k c", p=128)),
                     writes=[rwuv], dma=True)
                state = [(self.banks[0], self.banks[1]), (self.banks[2], self.banks[3])]
                first_tile = [True, True]
                n_tiles_total = sum(len(sc["order"]) for sc in scs)
                done = 0
                for sc in scs:
                    groups = []
                    for t in sc["own"]:
                        sub = seg.subs[t["j"]]
                        s = t["slot"]
                        tv, tr_ = self.next_tv()
                        for kc in range(4):
                            S.op("pe", I("transpose", out=tv[:, kc * 128:(kc + 1) * 128], in_=self.CKVB[:, sub, kc * 128:(kc + 1) * 128], identity=ident),
                                 reads=[self.RCKVB, self.RCM], writes=[tr_])
                        for kc in range(4):
                            self.copy_op(self.evac_engine(), self.CKVT[:, kc, s * 128:(s + 1) * 128], tv[:, kc * 128:(kc + 1) * 128], [tr_], [self.RCKVT])
                        tv2, tr2_ = self.next_tv()
                        S.op("pe", I("transpose", out=tv2[0:64, 0:128], in_=self.KRB[:, sub, :], identity=ident),
                             reads=[self.RKRB, self.RCM], writes=[tr2_])
                        self.copy_op(self.evac_engine(), self.KRT[0:64, s * 128:(s + 1) * 128], tv2[0:64, 0:128], [tr2_], [self.RKRT])
                        S.op("act", I("activation", out=self.XN[:, 0:64], in_=self.KRB[:, sub, :], func=AF.Square, accum_out=self.SSR[:, s:s + 1]),
                             reads=[self.RKRB], writes=[self.RXN, self.RSSR])
                    if sc["own"]:
                        groups.append((0, len(sc["own"])))
                    for ch in sc["hist"]:
                        c = ch["c"]
                        s0 = ch["slot0"]
                        ctok, rctok = self.rot(self.CTOK, "ctok")
                        krtok, rkrtok = self.rot(self.KRTOK, "krtok")
                        srcc, rc_ = seg.hist_src(self, l, "ckv", c)
                        srcr, rr_ = seg.hist_src(self, l, "kr", c)
                        S.op("pool", I("dma_start", out=ctok[:, :, :], in_=srcc.rearrange("(k p) c -> p k c", p=128)),
                             reads=[rc_], writes=[rctok], dma=True)
                        S.op("pool", I("dma_start", out=krtok[:, :, :], in_=srcr.rearrange("(k p) c -> p k c", p=128)),
                             reads=[rr_], writes=[rkrtok], dma=True)
                        for kc in range(4):
                            tv, tr_ = self.next_tv()
                            for k in range(4):
                                S.op("pe", I("transpose", out=tv[:, k * 128:(k + 1) * 128], in_=ctok[:, k, kc * 128:(kc + 1) * 128], identity=ident),
                                     reads=[rctok, self.RCM], writes=[tr_])
                            self.copy_op(self.evac_engine(), self.CKVT[:, kc, s0 * 128:(s0 + 4) * 128], tv[:, :], [tr_], [self.RCKVT])
                        tv2, tr2_ = self.next_tv()
                        for k in range(4):
                            S.op("pe", I("transpose", out=tv2[0:64, k * 128:(k + 1) * 128], in_=krtok[:, k, :], identity=ident),
                                 reads=[rkrtok, self.RCM], writes=[tr2_])
                        self.copy_op(self.evac_engine(), self.KRT[0:64, s0 * 128:(s0 + 4) * 128], tv2[0:64, :], [tr2_], [self.RKRT])
                        for k in range(4):
                            S.op("act", I("activation", out=self.XN[:, 0:64], in_=krtok[:, k, :], func=AF.Square, accum_out=self.SSR[:, s0 + k:s0 + k + 1]),
                                 reads=[rkrtok], writes=[self.RXN, self.RSSR])
                        groups.append((s0, 4))
                    stb, rstb = self.STB
                    for (s0, nt) in groups:
                        ncols = nt * 128
                        for i in range(2):
                            zb, rz = self.banks[4 + self.z_rot % 2]
                            self.z_rot += 1
                            for kc in range(4):
                                S.op("pe", I("matmul", zb[:, 0:ncols], lhsT=wuk[:, kc, i * 128:(i + 1) * 128], rhs=self.CKVT[:, kc, s0 * 128:s0 * 128 + ncols], start=(kc == 0), stop=(kc == 3)),
                                     reads=[rwuk, self.RCKVT], writes=[rz])
                            S.op("dve", I("tensor_copy", out=self.KST[:, i, s0 * 128:s0 * 128 + ncols], in_=zb[:, 0:ncols]),
                                 reads=[rz], writes=[self.RKST])
                            sq, rsq = self.rot(self.SQ, "sq")
                            S.op("act", I("activation", out=sq[:, 0:ncols], in_=zb[:, 0:ncols], func=AF.Square),
                                 reads=[rz], writes=[rsq])
                            for k in range(nt):
                                col = i * NSLOT + s0 + k
                                S.op("pe", I("matmul", stb[:, col:col + 1], lhsT=sq[:, k * 128:(k + 1) * 128], rhs=ones[:, 0:1], start=True, stop=True),
                                     reads=[rsq, self.RCM], writes=[rstb])
                        for k in range(nt):
                            zb, rz = self.banks[4 + self.z_rot % 2]
                            self.z_rot += 1
                            for kc in range(4):
                                S.op("pe", I("matmul", zb[:, 0:256], lhsT=self.CKVT[:, kc, (s0 + k) * 128:(s0 + k + 1) * 128], rhs=wuv[:, kc, :], start=(kc == 0), stop=(kc == 3)),
                                     reads=[rwuv, self.RCKVT], writes=[rz])
                            self.copy_op(self.evac_engine(), self.VST[:, s0 + k, :], zb[:, 0:256], [rz], [self.RVST])
                    nslots_used = max(s0 + nt for (s0, nt) in groups)
                    for i in range(2):
                        S.op("dve", I("tensor_tensor", out=self.RSTDK[:, i, 0:nslots_used], in0=stb[:, i * NSLOT:i * NSLOT + nslots_used], in1=self.SSR[:, 0:nslots_used], op=ALU.add),
                             reads=[rstb, self.RSSR], writes=[self.RRSTDK])
                    rk_all = self.RSTDK[:, :, 0:nslots_used]
                    S.op("dve", I("tensor_scalar", out=rk_all, in0=rk_all, scalar1=1.0 / QK, scalar2=EPS, op0=ALU.mult, op1=ALU.add),
                         reads=[self.RRSTDK], writes=[self.RRSTDK])
                    S.op("act", I("activation", out=rk_all, in_=rk_all, func=AF.Ln), reads=[self.RRSTDK], writes=[self.RRSTDK])
                    S.op("act", I("activation", out=rk_all, in_=rk_all, func=AF.Exp, scale=-0.5), reads=[self.RRSTDK], writes=[self.RRSTDK])
                    units = []
                    for t in sc["order"]:
                        done += 1
                        for i in range(2):
                            u = dict(t=t, i=i, h=hp * 2 + i, r=t["r"], c0=t["c0"], ncol=nq - t["c0"], s=t["slot"],
                                     ft=first_tile[i], last=(done == n_tiles_total))
                            first_tile[i] = False
                            units.append(u)

                    def m0(k):
                        u = units[k]
                        i, h, r, c0, ncol, sl = u["i"], u["h"], u["r"], u["c0"], u["ncol"], u["s"]
                        u["z"] = self.banks[4 + self.z_rot % 2]
                        self.z_rot += 1
                        zb, rz = u["z"]
                        S.op("pe", I("matmul", zb[0:r, 0:ncol], lhsT=self.KST[:, i, sl * 128:sl * 128 + r], rhs=self.QNT[:, h, seg.qc0 + c0:seg.qc0 + nq], start=True, stop=False),
                             reads=[self.RKST, self.RQNT], writes=[rz])
                        S.op("pe", I("matmul", zb[0:r, 0:ncol], lhsT=self.KRT[0:64, sl * 128:sl * 128 + r], rhs=self.QRT[0:64, h, seg.qc0 + c0:seg.qc0 + nq], start=False, stop=True),
                             reads=[self.RKRT, self.RQRT], writes=[rz])

                    def m1(k):
                        u = units[k]
                        i, r, ncol, sl = u["i"], u["r"], u["ncol"], u["s"]
                        zb, rz = u["z"]
                        u["P"] = self.rot(self.P, "P")
                        Pb, rP = u["P"]
                        S.op("act", I("activation", out=Pb[0:r, 0:ncol], in_=zb[0:r, 0:ncol], func=AF.Exp, scale=self.RSTDK[0:r, i, sl:sl + 1]),
                             reads=[rz, self.RRSTDK], writes=[rP])
                        if u["t"]["own"]:
                            dc = u["t"]["dc"]
                            S.op("dve", I("tensor_tensor", out=Pb[0:r, 0:dc], in0=Pb[0:r, 0:dc], in1=self.CM[0:r, 4, 0:dc], op=ALU.mult),
                                 reads=[rP, self.RCM], writes=[rP])

                    def m2(k):
                        u = units[k]
                        i, r, c0, ncol, sl = u["i"], u["r"], u["c0"], u["ncol"], u["s"]
                        Pb, rP = u["P"]
                        (sumb, rsum), (avb, rav) = state[i]
                        S.op("pe", I("matmul", avb[:, c0:c0 + ncol], lhsT=self.VST[0:r, sl, i * 128:(i + 1) * 128], rhs=Pb[0:r, 0:ncol], start=u["ft"], stop=u["last"], skip_group_check=True),
                             reads=[self.RVST, rP], writes=[rav])
                        S.op("pe", I("matmul", sumb[:, c0:c0 + ncol], lhsT=ones[0:r, :], rhs=Pb[0:r, 0:ncol], start=u["ft"], stop=u["last"], skip_group_check=True),
                             reads=[self.RCM, rP], writes=[rsum])
                    self.pipeline([(0, m0), (1, m1), (2, m2)], len(units))
                for i in range(2):
                    h = hp * 2 + i
                    (sumb, rsum), (avb, rav) = state[i]
                    Rb, rR = self.rot(self.R, "R")
                    S.op("dve", I("reciprocal", out=Rb[:, 0:nq], in_=sumb[:, 0:nq]), reads=[rsum], writes=[rR])
                    S.op("dve", I("tensor_tensor", out=self.MLO[:, h, seg.qc0:seg.qc0 + nq], in0=avb[:, 0:nq], in1=Rb[:, 0:nq], op=ALU.mult),
                         reads=[rav, rR], writes=[self.RMLO])

    def merge(self, ps, l):
        S = self.S
        T, nsub = ps.T, ps.nsub
        hact = lambda kc: self.HT[:, kc, 0:T]
        for cg in range(4):
            def evac_ymla(j, bt, br):
                S.op("act", I("activation", out=self.TMPA[:, j, 0:T], in_=bt[:, 0:T], func=AF.Copy), reads=[br], writes=[self.RTMPA])
            self.gemm("feat", lambda kc: self.MLO[:, kc, 0:T], [self.RMLO], self.w_mla_proj[l], 0, 1, cg * 512, 512, T, nsub, evac_ymla)

            def evac_gmla(j, bt, br):
                sg, rsg = self.rot(self.SG, "sg")
                S.op("act", I("activation", out=sg[:, 0:T], in_=bt[:, 0:T], func=AF.Sigmoid), reads=[br], writes=[rsg])
                S.op("dve", I("tensor_tensor", out=self.TMPA[:, j, 0:T], in0=self.TMPA[:, j, 0:T], in1=sg[:, 0:T], op=ALU.mult),
                     reads=[rsg, self.RTMPA], writes=[self.RTMPA])
            self.gemm("feat", hact, [self.RHT], self.w_in[l], 0, 2, C_GMLA + cg * 512, 512, T, nsub, evac_gmla)
            def evac_ysb(j, bt, br):
                S.op("act", I("activation", out=self.TMPB[:, j, 0:T], in_=bt[:, 0:T], func=AF.Copy), reads=[br], writes=[self.RTMPB])
            self.gemm("feat", lambda kc: self.SBO[:, kc, 0:T], [self.RSBO], self.w_sb_proj[l], 0, 1, cg * 512, 512, T, nsub, evac_ysb)

            def evac_gsb(j, bt, br, cg=cg):
                sg, rsg = self.rot(self.SG, "sg")
                S.op("act", I("activation", out=sg[:, 0:T], in_=bt[:, 0:T], func=AF.Sigmoid), reads=[br], writes=[rsg])
                S.op("dve", I("tensor_tensor", out=sg[:, 0:T], in0=self.TMPB[:, j, 0:T], in1=sg[:, 0:T], op=ALU.mult), reads=[self.RTMPB, rsg], writes=[rsg])
                S.op("dve", I("tensor_tensor", out=self.UT[:, cg * 4 + j, 0:T], in0=sg[:, 0:T], in1=self.TMPA[:, j, 0:T], op=ALU.add),
                     reads=[rsg, self.RTMPA], writes=[self.RUT])
            self.gemm("feat", hact, [self.RHT], self.w_in[l], 0, 2, C_GSB + cg * 512, 512, T, nsub, evac_gsb)
        for cg in range(4):
            def evac_o(sub, bt, br, cg=cg):
                xs_ = self.X[:, sub, cg * 512:(cg + 1) * 512]
                S.op("dve", I("tensor_tensor", out=xs_, in0=bt[:, :], in1=xs_, op=ALU.add), reads=[br, self.RX], writes=[self.RX])
            self.gemm("tok", lambda kc: self.UT[:, kc, 0:T], [self.RUT], self.w_o[l], 0, 2, cg * 512, 512, T, nsub, evac_o)

    def ffn(self, ps, l):
        S = self.S
        T, nsub = ps.T, ps.nsub
        hact = lambda kc: self.HT[:, kc, 0:T]
        for q in range(4):
            aT, raT = self.AT[q % 2]
            for cg in range(4):
                def evac_up(j, bt, br, cg=cg):
                    Eb, rE = self.rot(self.E, "E")
                    S.op("act", I("activation", out=Eb[:, 0:T], in_=bt[:, 0:T], func=AF.Relu), reads=[br], writes=[rE])
                    S.op("dve", I("tensor_tensor", out=aT[:, cg * 4 + j, 0:T], in0=Eb[:, 0:T], in1=Eb[:, 0:T], op=ALU.mult), reads=[rE], writes=[raT])
                self.gemm("feat", hact, [self.RHT], self.w_up[l], 0, 2, q * 2048 + cg * 512, 512, T, nsub, evac_up)
            for cg in range(4):
                def evac_dn(sub, bt, br, cg=cg):
                    xs_ = self.X[:, sub, cg * 512:(cg + 1) * 512]
                    S.op("dve", I("tensor_tensor", out=xs_, in0=bt[:, :], in1=xs_, op=ALU.add), reads=[br, self.RX], writes=[self.RX])
                self.gemm("tok", lambda kc: aT[:, kc, 0:T], [raT], self.w_down[l], q * 2048, 2, cg * 512, 512, T, nsub, evac_dn)

    def load_consts(self):
        S = self.S
        S.op("pool", I("dma_start", out=self.CM[:, :, :], in_=self.c_mats.rearrange("m p c -> p m c")), writes=[self.RCM], dma=True)
        S.op("sp", I("dma_start", out=self.MSB[:, :], in_=self.c_mats[3]), writes=[self.RMSB], dma=True)
        for l in range(DEPTH):
            S.op("sp", I("dma_start", out=self.G1T[:, l, :], in_=self.norm1_g[l].rearrange("(k p) -> p k", p=128), allow_slow_non_contiguous=True), writes=[self.RG], dma=True)
            S.op("sp", I("dma_start", out=self.G2T[:, l, :], in_=self.norm2_g[l].rearrange("(k p) -> p k", p=128), allow_slow_non_contiguous=True), writes=[self.RG], dma=True)
            S.op("sp", I("dma_start", out=self.GKV[:, l, :], in_=self.kv_norm_g[l:l + 1, :].to_broadcast([128, LAT])), writes=[self.RG], dma=True)
            for ci, (src, a, b) in enumerate(((self.q_norm_g, 0, 128), (self.k_norm_g, 0, 128), (self.q_norm_g, 128, 192), (self.k_norm_g, 128, 192))):
                S.op("sp", I("dma_start", out=self.GQK[0:b - a, l, ci:ci + 1], in_=src[l, a:b].rearrange("(p o) -> p o", o=1), allow_slow_non_contiguous=True),
                     writes=[self.RG], dma=True)
            sc = float(QK) ** -0.5
            S.op("dve", I("scalar_tensor_tensor", out=self.CQ[:, l, 0:1], in0=self.GQK[:, l, 0:1], scalar=sc, in1=self.GQK[:, l, 1:2], op0=ALU.mult, op1=ALU.mult),
                 reads=[self.RG], writes=[self.RCQ])
            S.op("dve", I("scalar_tensor_tensor", out=self.CQ[0:64, l, 1:2], in0=self.GQK[0:64, l, 2:3], scalar=sc, in1=self.GQK[0:64, l, 3:4], op0=ALU.mult, op1=ALU.mult),
                 reads=[self.RG], writes=[self.RCQ])

    def run_pass(self, ps):
        S = self.S
        ps.load_x(self)
        for l in range(self.n_layers):
            self.norm_to_hT(ps, self.G1T, l)
            self.phase_switch([self.RQT, self.RSBK, self.RSBV], self.arena_regions)
            self.phase_switch([self.RKST, self.RVST], self.staging_regions)
            self.phase_switch(self.ls_regions, [])
            self.sb_proj(ps, l)
            self.phase_switch([self.G[0][1], self.G[1][1]], [self.RXN])
            self.sb_attention(ps, l)
            self.phase_switch([self.RXN], [self.G[0][1], self.G[1][1]])
            self.phase_switch([self.RQNT, self.RQRT, self.RCKVB, self.RKRB, self.WUK[0][1], self.WUK[1][1], self.WUV[0][1], self.WUV[1][1]], self.arena_regions)
            self.phase_switch(self.ls_regions, [])
            self.mla_proj(ps, l)
            self.mla_attention(ps, l)
            self.phase_switch([self.RUT], self.arena_regions)
            self.phase_switch([self.RTMPA, self.RTMPB, self.SG[0][1], self.SG[1][1]], self.staging_regions)
            self.merge(ps, l)
            self.norm_to_hT(ps, self.G2T, l)
            self.phase_switch([self.AT[0][1], self.AT[1][1]], self.staging_regions)
            self.ffn(ps, l)
        ps.store_y(self)

    def build(self):
        self.alloc()
        passes = [make_pass(self, cfg) for cfg in self.passes_cfg]
        self.slab_init(len(passes))
        self.load_consts()
        for ps in passes:
            self.run_pass(ps)
        self.S.finish()
        self.S.emit()
        return self.nc


def hist_region(B, key):
    r = B.hist_regions.get(key)
    if r is None:
        r = Region("hist" + str(key))
        B.hist_regions[key] = r
    return r


def make_pass(B, cfg):
    ps = Pass()
    kind = cfg[0]
    ps.kind = kind
    if kind == "prompt":
        _, pb, tb = cfg
        ps.T, ps.nsub = 512, 4
        seg = Seg()
        seg.subs = [0, 1, 2, 3]
        seg.rows = 128
        seg.nq = 512
        seg.qc0 = 0
        seg.hist_n = 512 * tb
        outs = {"k": B.pk, "v": B.pv, "ckv": B.pckv, "kr": B.pkr}

        def hist_src(B_, l, name, c):
            return outs[name][l, pb, c * 512:(c + 1) * 512, :], hist_region(B_, (name, l, pb, c))
        seg.hist_src = hist_src
        ps.segs = [seg]

        def load_x(B_):
            src = B_.xp[pb, tb * 512:(tb + 1) * 512, :].rearrange("(s p) d -> p s d", p=128)
            for s in range(4):
                B_.S.op("sp", I("dma_start", out=B_.X[:, s, :], in_=B_.xp[pb, tb * 512 + s * 128:tb * 512 + (s + 1) * 128, :]), writes=[B_.RX], dma=True)
            B_.S.op("sp", I("dma_start", out=B_.ROPE[:, :, :], in_=B_.rope_p[tb * 512:(tb + 1) * 512, :].rearrange("(s p) c -> p s c", p=128)), writes=[B_.RROPE], dma=True)
        ps.load_x = load_x

        def store_y(B_):
            for s in range(4):
                B_.S.op("sp", I("dma_start", out=B_.yp[pb, tb * 512 + s * 128:tb * 512 + (s + 1) * 128, :], in_=B_.X[:, s, :]), reads=[B_.RX], cowrites=[B_.Rdram_out], dma=True, is_store=True)
        ps.store_y = store_y

        def store_kv(B_, l, name, sub, zo, rzo, c0, ncols):
            dst = outs[name][l, pb, tb * 512 + sub * 128:tb * 512 + (sub + 1) * 128, c0:c0 + ncols]
            B_.S.op("sp", I("dma_start", out=dst, in_=zo[:, 0:ncols]), reads=[rzo], cowrites=[hist_region(B_, (name, l, pb, tb))], dma=True, is_store=True)
        ps.store_kv = store_kv
    else:
        ps.T, ps.nsub = 256, 2
        caches = {"k": B.csk, "v": B.csv, "ckv": B.cckv, "kr": B.ckr}
        outs = {"k": B.sk, "v": B.sv, "ckv": B.sckv, "kr": B.skr}
        Rin = Region("cache_inputs")
        ps.segs = []
        for s in range(2):
            seg = Seg()
            seg.subs = [s]
            seg.rows = 64
            seg.nq = 64
            seg.qc0 = s * 128
            seg.hist_n = PAST

            def hist_src(B_, l, name, c, s=s):
                return caches[name][l, s, c * 512:(c + 1) * 512, :], Rin
            seg.hist_src = hist_src
            ps.segs.append(seg)

        def load_x(B_):
            B_.S.op("dve", I("memset", B_.X[:, 0:2, :], 0.0), writes=[B_.RX])
            B_.S.op("dve", I("memset", B_.SBO[:, :, 0:256], 0.0), writes=[B_.RSBO])
            B_.S.op("dve", I("memset", B_.MLO[:, :, 0:256], 0.0), writes=[B_.RMLO])
            for s in range(2):
                B_.S.op("sp", I("dma_start", out=B_.X[0:64, s, :], in_=B_.xs[s, :, :]), writes=[B_.RX], dma=True)
                B_.S.op("sp", I("dma_start", out=B_.ROPE[:, s, :], in_=B_.rope_s[:, :]), writes=[B_.RROPE], dma=True)
        ps.load_x = load_x

        def store_y(B_):
            for s in range(2):
                B_.S.op("sp", I("dma_start", out=B_.ys[s, :, :], in_=B_.X[0:64, s, :]), reads=[B_.RX], cowrites=[B_.Rdram_out], dma=True, is_store=True)
        ps.store_y = store_y

        def store_kv(B_, l, name, sub, zo, rzo, c0, ncols):
            dst = outs[name][l, sub, :, c0:c0 + ncols]
            B_.S.op("sp", I("dma_start", out=dst, in_=zo[0:64, 0:ncols]), reads=[rzo], cowrites=[B_.Rdram_out], dma=True, is_store=True)
        ps.store_kv = store_kv
    return ps


def const_inputs():
    k = np.arange(128)[:, None]
    q = np.arange(128)[None, :]
    ident = np.eye(128, dtype=np.float32)
    trige = (k >= q).astype(np.float32)
    trilt = (k < q).astype(np.float32)
    mask_sb = (k < q).astype(np.float32)
    mask_mla = ((k // 64) <= (q // 64)).astype(np.float32)
    ones = np.ones((128, 128), np.float32)
    c_mats = np.stack([ident, trige, trilt, mask_sb, mask_mla, ones]).astype(np.float32)
    half = ROPE // 2
    inv_freq = (np.float32(10000.0) ** (-np.arange(half, dtype=np.float32) / np.float32(half))).astype(np.float32)

    def table(pos):
        ang = pos.astype(np.float32)[:, None] * inv_freq[None, :]
        c = np.cos(ang).astype(np.float32)
        s = np.sin(ang).astype(np.float32)
        return np.concatenate([c, c, s, s], axis=1).astype(np.float32)
    rope_p = table(np.arange(SEQ))
    rope_s = np.zeros((128, 128), np.float32)
    rope_s[:DSEQ] = table(PAST + np.arange(DSEQ))
    return c_mats, rope_p, rope_s


ALL_PASSES = [("prompt", pb, tb) for pb in range(2) for tb in range(4)] + [("sample",)]
_prog_cache = {}


def get_program(passes_cfg, n_layers=DEPTH):
    key = (tuple(passes_cfg), n_layers)
    if key not in _prog_cache:
        global _last_builder
        _last_builder = Builder(list(passes_cfg), n_layers)
        _prog_cache[key] = _last_builder.build()
    return _prog_cache[key]


def run(inputs, cores=None, passes_cfg=None, n_layers=DEPTH):
    cores = list(range(NCORES)) if cores is None else cores
    passes_cfg = ALL_PASSES if passes_cfg is None else passes_cfg
    nc = get_program(passes_cfg, n_layers)
    c_mats, rope_p, rope_s = const_inputs()
    f = lambda a: np.ascontiguousarray(np.asarray(a, dtype=np.float32))
    shared = {k: f(inputs[k]) for k in ("norm1_g", "norm2_g", "q_norm_g", "k_norm_g", "kv_norm_g", "w_in", "w_uk", "w_uv",
                                        "w_sb_proj", "w_mla_proj", "w_o", "w_up", "w_down")}
    shared.update(c_mats=c_mats, rope_p=rope_p, rope_s=rope_s)
    in_maps = []
    for c in cores:
        m = dict(shared)
        b0 = 2 * c
        m["xp"] = f(inputs["x_prompt"][b0:b0 + 2])
        m["xs"] = f(inputs["x_sample"][b0:b0 + 2])
        m["csk"] = f(np.asarray(inputs["cache_sb_k"])[:, b0:b0 + 2].reshape(DEPTH, 2, PAST, 1024))
        m["csv"] = f(np.asarray(inputs["cache_sb_v"])[:, b0:b0 + 2].reshape(DEPTH, 2, PAST, 1024))
        m["cckv"] = f(np.asarray(inputs["cache_mla_ckv"])[:, b0:b0 + 2])
        m["ckr"] = f(np.asarray(inputs["cache_mla_krope"])[:, b0:b0 + 2])
        in_maps.append(m)
    res = run_bass_kernel_spmd(nc, in_maps, core_ids=list(range(len(cores))))
    return res.results


def kernel(**inputs):
    results = run(inputs)
    B = 2 * NCORES
    cat1 = lambda name: np.concatenate([r[name] for r in results], axis=0)
    cat2 = lambda name: np.concatenate([r[name] for r in results], axis=1)
    y_prompt = cat1("yp").astype(np.float32)
    y_sample = cat1("ys").astype(np.float32)
    p_k = cat2("pk").reshape(DEPTH, B, SEQ, NH, HD).astype(np.float32)
    p_v = cat2("pv").reshape(DEPTH, B, SEQ, NH, HD).astype(np.float32)
    p_ckv = cat2("pckv").astype(np.float32)
    p_kr = cat2("pkr").astype(np.float32)
    s_k = cat2("sk").reshape(DEPTH, B, DSEQ, NH, HD).astype(np.float32)
    s_v = cat2("sv").reshape(DEPTH, B, DSEQ, NH, HD).astype(np.float32)
    s_ckv = cat2("sckv").astype(np.float32)
    s_kr = cat2("skr").astype(np.float32)
    return (y_prompt, y_sample, p_k, p_v, p_ckv, p_kr, s_k, s_v, s_ckv, s_kr)
```

```python
import contextlib
import numpy as np
import concourse.bass as bass
import concourse.mybir as mybir
from concourse.bass_utils import run_bass_kernel_spmd

F32 = mybir.dt.float32
BF16 = mybir.dt.bfloat16
AF = mybir.ActivationFunctionType
ALU = mybir.AluOpType

NCORES = 8
D = 2048
NH = 8
HD = 128
LAT = 512
ROPE = 64
QK = 192
INW = 9280
DFF = 8192
PAST = 4096
SEQ = 2048
DSEQ = 64
DEPTH = 2
EPS = 1e-6
C_SBQ, C_SBK, C_SBV, C_MQ, C_CKV, C_KR, C_GSB, C_GMLA = 0, 1024, 2048, 3072, 4608, 5120, 5184, 7232
NSLOT = 17
NSLAB = 3
NDMASEM = 24


def I(method, *args, **kw):
    return lambda e: getattr(e, method)(*args, **kw)


class Region:
    __slots__ = ("name", "last_w", "readers", "cow", "psum")

    def __init__(self, name, psum=False):
        self.name = name
        self.last_w = None
        self.readers = []
        self.cow = []
        self.psum = psum


class Op:
    __slots__ = ("eng", "fn", "deps", "is_dma", "needs_inc", "semval", "dma_id", "idx")


class Sched:
    ENGS = ("pe", "act", "dve", "pool", "sp")

    def __init__(self, nc):
        self.nc = nc
        self.ops = []
        self.dma_ops = {"sp": [], "pool": [], "act": []}
        self.store_ops = []

    def op(self, eng, fn, reads=(), writes=(), dma=False, is_store=False, cowrites=()):
        o = Op()
        o.eng = eng
        o.fn = fn
        o.is_dma = dma
        o.needs_inc = False
        o.semval = None
        o.dma_id = None
        o.idx = len(self.ops)
        deps = {}
        for r in reads:
            if r.last_w is not None:
                deps[r.last_w.idx] = (r.last_w, "raw")
            for cw in r.cow:
                deps[cw.idx] = (cw, "raw")
            if r.psum:
                for rd in r.readers:
                    if rd.eng != eng:
                        deps.setdefault(rd.idx, (rd, "rar"))
        for w in writes:
            if w.last_w is not None:
                deps.setdefault(w.last_w.idx, (w.last_w, "waw"))
            for cw in w.cow:
                deps.setdefault(cw.idx, (cw, "waw"))
            for rd in self._last_readers(w):
                deps.setdefault(rd.idx, (rd, "war"))
        for w in cowrites:
            if w.last_w is not None:
                deps.setdefault(w.last_w.idx, (w.last_w, "waw"))
            for rd in self._last_readers(w):
                deps.setdefault(rd.idx, (rd, "war"))
        if dma:
            lst = self.dma_ops[eng]
            o.dma_id = len(lst)
            if o.dma_id >= NDMASEM:
                prev = lst[o.dma_id - NDMASEM]
                deps.setdefault(prev.idx, (prev, "raw"))
            lst.append(o)
            if is_store:
                self.store_ops.append(o)
        final = []
        for d, kind in deps.values():
            if not d.is_dma and d.eng == eng and not dma:
                if eng == "pe":
                    continue
            if not d.is_dma:
                d.needs_inc = True
            final.append(d)
        o.deps = final
        for r in reads:
            r.readers.append(o)
        for w in writes:
            w.last_w = o
            w.readers = []
            w.cow = []
        for w in cowrites:
            w.cow.append(o)
        self.ops.append(o)
        return o

    @staticmethod
    def _last_readers(w):
        last = {}
        out = []
        for rd in w.readers:
            if rd.is_dma:
                out.append(rd)
            else:
                last[rd.eng] = rd
        return out + list(last.values())

    def finish(self):
        deps = list(self.store_ops)
        last = {}
        for o in self.ops:
            if not o.is_dma:
                last[o.eng] = o
        o = Op()
        o.eng = "sp"
        o.fn = None
        o.is_dma = False
        o.needs_inc = False
        o.semval = None
        o.dma_id = None
        o.idx = len(self.ops)
        for e, lo in last.items():
            if e != "sp":
                lo.needs_inc = True
                deps.append(lo)
        o.deps = deps
        self.ops.append(o)

    def emit(self):
        nc = self.nc
        counters = {e: 0 for e in self.ENGS}
        for o in self.ops:
            if not o.is_dma and o.needs_inc:
                counters[o.eng] += 1
                o.semval = counters[o.eng]
        per_eng = {e: [] for e in self.ENGS}
        for o in self.ops:
            per_eng[o.eng].append(o)
        with contextlib.ExitStack() as st:
            esem = {e: st.enter_context(nc.semaphore("s_" + e)) for e in self.ENGS}
            dsem = {q: [st.enter_context(nc.semaphore("d%s%d" % (q, i))) for i in range(NDMASEM)] for q in ("sp", "pool")}
            block = st.enter_context(nc.Block())

            def run_stream(e, eng):
                seen = {}
                for o in per_eng[e]:
                    for d in o.deps:
                        if d.is_dma:
                            key = ("d", d.eng, d.dma_id % NDMASEM)
                            sem = dsem[d.eng][d.dma_id % NDMASEM]
                            val = 16 * (d.dma_id // NDMASEM + 1)
                        else:
                            key = d.eng
                            sem = esem[d.eng]
                            val = d.semval
                        if seen.get(key, 0) < val:
                            eng.wait_ge(sem, val)
                            seen[key] = val
                    if o.fn is None:
                        continue
                    ins = o.fn(eng)
                    if o.is_dma:
                        ins.then_inc(dsem[o.eng][o.dma_id % NDMASEM], 16)
                    elif o.needs_inc:
                        ins.then_inc(esem[e], 1)

            @block.tensor
            def _(eng):
                run_stream("pe", eng)

            @block.scalar
            def _(eng):
                run_stream("act", eng)

            @block.vector
            def _(eng):
                run_stream("dve", eng)

            @block.gpsimd
            def _(eng):
                run_stream("pool", eng)

            @block.sync
            def _(eng):
                run_stream("sp", eng)


class Seg:
    pass


class Pass:
    pass


class Builder:
    def __init__(self, passes_cfg, n_layers=DEPTH):
        self.n_layers = n_layers
        self.nc = nc = bass.Bass("TRN2", target_bir_lowering=False)
        self.S = Sched(nc)
        self.passes_cfg = passes_cfg
        di = lambda name, shape: nc.dram_tensor(name, list(shape), F32, kind="ExternalInput").ap()
        do = lambda name, shape: nc.dram_tensor(name, list(shape), F32, kind="ExternalOutput").ap()
        self.xp = di("xp", (2, SEQ, D))
        self.xs = di("xs", (2, DSEQ, D))
        self.csk = di("csk", (DEPTH, 2, PAST, 1024))
        self.csv = di("csv", (DEPTH, 2, PAST, 1024))
        self.cckv = di("cckv", (DEPTH, 2, PAST, LAT))
        self.ckr = di("ckr", (DEPTH, 2, PAST, ROPE))
        self.norm1_g = di("norm1_g", (DEPTH, D))
        self.norm2_g = di("norm2_g", (DEPTH, D))
        self.q_norm_g = di("q_norm_g", (DEPTH, QK))
        self.k_norm_g = di("k_norm_g", (DEPTH, QK))
        self.kv_norm_g = di("kv_norm_g", (DEPTH, LAT))
        self.w_in = di("w_in", (DEPTH, D, INW))
        self.w_uk = di("w_uk", (DEPTH, LAT, 1024))
        self.w_uv = di("w_uv", (DEPTH, LAT, 1024))
        self.w_sb_proj = di("w_sb_proj", (DEPTH, 1024, D))
        self.w_mla_proj = di("w_mla_proj", (DEPTH, 1024, D))
        self.w_o = di("w_o", (DEPTH, D, D))
        self.w_up = di("w_up", (DEPTH, D, DFF))
        self.w_down = di("w_down", (DEPTH, DFF, D))
        self.c_mats = di("c_mats", (6, 128, 128))
        self.rope_p = di("rope_p", (SEQ, 128))
        self.rope_s = di("rope_s", (128, 128))
        self.yp = do("yp", (2, SEQ, D))
        self.ys = do("ys", (2, DSEQ, D))
        self.pk = do("pk", (DEPTH, 2, SEQ, 1024))
        self.pv = do("pv", (DEPTH, 2, SEQ, 1024))
        self.pckv = do("pckv", (DEPTH, 2, SEQ, LAT))
        self.pkr = do("pkr", (DEPTH, 2, SEQ, ROPE))
        self.sk = do("sk", (DEPTH, 2, DSEQ, 1024))
        self.sv = do("sv", (DEPTH, 2, DSEQ, 1024))
        self.sckv = do("sckv", (DEPTH, 2, DSEQ, LAT))
        self.skr = do("skr", (DEPTH, 2, DSEQ, ROPE))
        self.hist_regions = {}
        self.Rdram_out = Region("dram_out_misc")
        self.wscr = nc.dram_tensor("wscr", [DEPTH * 120, 128, 4096], BF16, kind="Internal").ap()
        self.wscr_regions = [Region("wscr%d" % i) for i in range(DEPTH * 120)]
        self.krscr = nc.dram_tensor("krscr", [4, 128, 128], F32, kind="Internal").ap()
        self.krscr_regions = [Region("krscr%d" % i) for i in range(4)]
        self.kr_rot = 0

    def alloc(self):
        nc = self.nc
        total = 212736
        self.arena = nc.alloc_sbuf_tensor("arena", [128, total // 2], BF16)
        self._off = 0

        def take(nbytes):
            o = self._off
            self._off += (nbytes + 63) // 64 * 64
            assert self._off <= total, ("SBUF overflow", self._off)
            return o
        self.take = take

        def view(off, shape, dt):
            n = 1
            for s in shape[1:]:
                n *= s
            esz = 4 if dt == F32 else 2
            a = self.arena[0:shape[0], off // 2: off // 2 + n * esz // 2]
            if dt == F32:
                a = a.bitcast(F32)
            if len(shape) == 3:
                a = a.rearrange("p (a b) -> p a b", b=shape[2])
            elif len(shape) == 4:
                a = a.rearrange("p (a b c) -> p a b c", b=shape[2], c=shape[3])
            return a
        self.view = view

        def buf(shape, dt, name, off=None):
            n = 1
            for s in shape[1:]:
                n *= s
            nbytes = n * (4 if dt == F32 else 2)
            if off is None:
                off = take(nbytes)
            return view(off, shape, dt), Region(name), off, nbytes

        self.X, self.RX, _, _ = buf([128, 4, D], F32, "X")
        self.HT, self.RHT, o_ht, _ = buf([128, 16, 512], BF16, "HT")
        self.HTJ = [view(o_ht + i * 4096, [128, D], BF16) for i in range(4)]
        self.slabs = []
        for i in range(NSLAB):
            v, r, _, _ = buf([128, 8, 512], BF16, "slab%d" % i)
            self.slabs.append((v, r))
        self.CM, self.RCM, _, _ = buf([128, 6, 128], BF16, "cmats")
        self.MSB, self.RMSB, _, _ = buf([128, 128], F32, "mask_sb_f32")
        self.G1T, self.RG, _, _ = buf([128, DEPTH, 16], F32, "g1T")
        self.G2T, _, _, _ = buf([128, DEPTH, 16], F32, "g2T")
        self.GKV, _, _, _ = buf([128, DEPTH, LAT], F32, "gkv")
        self.GQK, _, _, _ = buf([128, DEPTH, 4], F32, "gqk")
        self.CQ, self.RCQ, _, _ = buf([128, DEPTH, 2], F32, "cq")
        self.ROPE, self.RROPE, _, _ = buf([128, 4, 128], F32, "rope")
        self.ST, self.RST, _, _ = buf([128, 64], F32, "stats")
        self.ST2, self.RST2, _, _ = buf([128, 64], F32, "stats2")
        self.STQ = [buf([128, 8], F32, "stq%d" % i)[:2] for i in range(4)]
        self.SBO, self.RSBO, _, _ = buf([128, 8, 512], BF16, "sb_outT")
        self.MLO, self.RMLO, _, _ = buf([128, 8, 512], BF16, "mla_outT")
        self.E = [buf([128, 512], F32, "E%d" % i)[:2] for i in range(5)]
        self.L = [buf([128, 512], BF16, "L%d" % i)[:2] for i in range(3)]
        self.A = [buf([128, 512], BF16, "A%d" % i)[:2] for i in range(3)]
        self.LS = [[buf([128, 512], BF16, "LS%d_%d" % (i, k))[:2] for k in range(2)] for i in range(2)]
        self.ls_rot = [0, 0]
        self.c_rot = 0
        self.P = self.L
        self.SQ = self.A
        self.R = self.E
        self.ZOUT = [buf([128, 512], F32, "ZOUT%d" % i)[:2] for i in range(2)]
        self.QB = []
        self.ZZ = []
        for i in range(2):
            v, r, o, _ = buf([128, 2, QK], BF16, "QB%d" % i)
            self.QB.append((v, r))
            self.ZZ.append((view(o, [128, 128], F32), r))
        self.RT = [buf([128, 2, 64], F32, "RT%d" % i)[:2] for i in range(2)]
        self.RT2 = [buf([128, 2, 64], F32, "RTb%d" % i)[:2] for i in range(2)]
        self.XN, self.RXN, o_xn, _ = buf([128, D], BF16, "XN")
        self.G = [(view(o_xn + i * 2048, [128, 512], F32), Region("G%d" % i)) for i in range(2)]
        o_ls = take(9216)
        self.KTOK = [(view(o_ls + i * 2048, [128, 4, 256], BF16), Region("ktok%d" % i)) for i in range(2)]
        self.CTOK = [(view(o_ls + i * 4096, [128, 4, 512], BF16), Region("ctok%d" % i)) for i in range(2)]
        self.KRTOK = [(view(o_ls + 8192 + i * 512, [128, 4, 64], BF16), Region("krtok%d" % i)) for i in range(2)]
        self.R_LS = Region("loadstage")
        o_ar = take(29184)
        self.R_AR = Region("arena_phase")
        self.QT = view(o_ar, [128, 8, 512], BF16); self.RQT = Region("qT")
        self.SBK = view(o_ar + 8192, [128, 4, 1024], BF16); self.RSBK = Region("sbk")
        self.SBV = view(o_ar + 16384, [128, 4, 1024], BF16); self.RSBV = Region("sbv")
        self.QNT = view(o_ar, [128, 8, 512], BF16); self.RQNT = Region("qnT")
        self.QRT = view(o_ar + 8192, [128, 8, 512], BF16); self.RQRT = Region("qrT")
        self.CKVB = view(o_ar + 16384, [128, 4, 512], BF16); self.RCKVB = Region("ckvb")
        self.KRB = view(o_ar + 20480, [128, 4, 64], BF16); self.RKRB = Region("krb")
        self.WUK = [(view(o_ar + 20992 + i * 2048, [128, 4, 256], BF16), Region("wuk%d" % i)) for i in range(2)]
        self.WUV = [(view(o_ar + 25088 + i * 2048, [128, 4, 256], BF16), Region("wuv%d" % i)) for i in range(2)]
        self.UT = view(o_ar, [128, 16, 512], BF16); self.RUT = Region("uT")
        self.arena_regions = [self.RQT, self.RSBK, self.RSBV, self.RQNT, self.RQRT, self.RCKVB, self.RKRB,
                              self.WUK[0][1], self.WUK[1][1], self.WUV[0][1], self.WUV[1][1], self.RUT,
]
        o_st = take(40448)
        NK = NSLOT * 128
        self.KST = view(o_st, [128, 2, NK], BF16); self.RKST = Region("kst")
        self.VST = view(o_st + 8704, [128, NSLOT, 256], BF16); self.RVST = Region("vst")
        self.CKVT = view(o_st + 17408, [128, 4, NK], BF16); self.RCKVT = Region("ckvT")
        self.KRT = view(o_st + 34816, [128, NK], BF16); self.RKRT = Region("krT")
        self.RSTDK = view(o_st + 39168, [128, 2, NSLOT], F32); self.RRSTDK = Region("rstdk")
        self.SSR = view(o_st + 39168 + 192, [128, NSLOT], F32); self.RSSR = Region("ssr")
        self.AT = [(view(o_st + i * 16384, [128, 16, 512], BF16), Region("aT%d" % i)) for i in range(2)]
        self.TMPA = view(o_st, [128, 4, 512], F32); self.RTMPA = Region("tmpA")
        self.TMPB = view(o_st + 8192, [128, 4, 512], F32); self.RTMPB = Region("tmpB")
        self.SG = [(view(o_st + 16384 + i * 2048, [128, 512], F32), Region("sg%d" % i)) for i in range(2)]
        self.staging_regions = [self.RKST, self.RVST, self.RCKVT, self.RKRT, self.RRSTDK, self.RSSR,
                                self.AT[0][1], self.AT[1][1], self.RTMPA, self.RTMPB, self.SG[0][1], self.SG[1][1]]
        self.ls_regions = [r for (_, r) in self.KTOK + self.CTOK + self.KRTOK]
        self.banks = []
        for i in range(8):
            t = nc.alloc_psum_tensor("bank%d" % i, [128, 512], F32)
            self.banks.append((t, Region("bank%d" % i, psum=True)))
        self.TV = [(self.banks[6][0][:, :].bitcast(BF16)[:, 0:512], self.banks[6][1]),
                   (self.banks[7][0][:, :].bitcast(BF16)[:, 0:512], self.banks[7][1])]
        self.STB = (self.banks[7][0][:, 256:512], self.banks[7][1])
        self.gemm_rot = 0
        self.z_rot = 0
        self.t_rot = 0
        self.evac_rot = 0
        self.work_rot = {}

    @staticmethod
    def pipeline(stages, n):
        depth = max(off for off, _ in stages)
        for k in range(n + depth):
            for off, fn in stages:
                if 0 <= k - off < n:
                    fn(k - off)

    def rot(self, lst, key):
        i = self.work_rot.get(key, 0)
        self.work_rot[key] = i + 1
        return lst[i % len(lst)]

    def phase_switch(self, regions_new, regions_old):
        S = self.S
        S.op("dve", I("memset", self.ST2[0:1, 63:64], 0.0), reads=[], writes=list(regions_new) + list(regions_old) + [self.RST2])

    def bank(self, i):
        return self.banks[i]

    def next_gemm_banks(self, n):
        res = []
        for _ in range(n):
            res.append(self.banks[self.gemm_rot % 6])
            self.gemm_rot += 1
        return res

    def next_tv(self):
        self.t_rot += 1
        return self.TV[self.t_rot % 2]

    def evac_engine(self):
        self.evac_rot += 1
        return "act" if self.evac_rot % 2 == 0 else "dve"

    def copy_op(self, eng, out, in_, reads, writes, scale=None):
        S = self.S
        if eng == "act":
            if scale is None:
                S.op("act", I("activation", out=out, in_=in_, func=AF.Copy), reads=reads, writes=writes)
            else:
                S.op("act", I("activation", out=out, in_=in_, func=AF.Copy, scale=scale), reads=reads, writes=writes)
        else:
            if scale is None:
                S.op("dve", I("tensor_copy", out=out, in_=in_), reads=reads, writes=writes)
            else:
                S.op("dve", I("tensor_scalar", out=out, in0=in_, scalar1=scale, scalar2=None, op0=ALU.mult), reads=reads, writes=writes)

    def rstd_op(self, out, ss, n, reads, writes, tmp):
        S = self.S
        S.op("dve", I("tensor_scalar", out=tmp, in0=ss, scalar1=1.0 / n, scalar2=EPS, op0=ALU.mult, op1=ALU.add),
             reads=reads, writes=writes)
        S.op("act", I("activation", out=tmp, in_=tmp, func=AF.Ln), reads=writes, writes=writes)
        S.op("act", I("activation", out=out, in_=tmp, func=AF.Exp, scale=-0.5), reads=writes, writes=writes)

    def store_kr_rows(self, zo, rzo, nrows, dst2d, dst_region, cow):
        S = self.S
        slot = self.kr_rot % 4
        self.kr_rot += 1
        scr = self.krscr[slot]
        rscr = self.krscr_regions[slot]
        zz, rzz = self.ZZ[slot % 2]
        h = nrows // 2
        S.op("sp", I("dma_start", out=scr[0:nrows, 0:ROPE], in_=zo[0:nrows, 0:ROPE]), reads=[rzo], writes=[rscr], dma=True)
        S.op("sp", I("dma_start", out=zz[0:h, :].rearrange("p (j d) -> p j d", j=2),
                     in_=scr[0:nrows, 0:ROPE].rearrange("(p j) d -> p j d", j=2)), reads=[rscr], writes=[rzz], dma=True)
        kw = dict(cowrites=[dst_region]) if cow else dict(cowrites=[dst_region])
        S.op("sp", I("dma_start", out=dst2d.rearrange("(p j) d -> p (j d)", j=2), in_=zz[0:h, :]), reads=[rzz], dma=True, is_store=True, **kw)

    def slab_plan_layer(self, l):
        plan = []
        w_in = self.w_in

        def add(W2d, r0, nk, c0, nc_):
            plan.append((W2d, r0, nk, c0, nc_))
        for cg in range(6):
            for kh in range(2):
                add(w_in[l], kh * 1024, 8, cg * 512, 512)
        for cg in range(4):
            for kh in range(2):
                add(w_in[l], kh * 1024, 8, C_MQ + cg * 384, 384)
        for kh in range(2):
            add(w_in[l], kh * 1024, 8, C_CKV, 512)
        for kh in range(2):
            add(w_in[l], kh * 1024, 8, C_KR, 64)
        for cg in range(4):
            add(self.w_mla_proj[l], 0, 8, cg * 512, 512)
            for kh in range(2):
                add(w_in[l], kh * 1024, 8, C_GMLA + cg * 512, 512)
            add(self.w_sb_proj[l], 0, 8, cg * 512, 512)
            for kh in range(2):
                add(w_in[l], kh * 1024, 8, C_GSB + cg * 512, 512)
        for cg in range(4):
            for kh in range(2):
                add(self.w_o[l], kh * 1024, 8, cg * 512, 512)
        for q in range(4):
            for cg in range(4):
                for kh in range(2):
                    add(self.w_up[l], kh * 1024, 8, q * 2048 + cg * 512, 512)
            for cg in range(4):
                for kh in range(2):
                    add(self.w_down[l], q * 2048 + kh * 1024, 8, cg * 512, 512)
        return plan

    def slab_init(self, n_passes):
        self.slab_list = []
        for p in range(n_passes):
            for l in range(self.n_layers):
                plan = self.slab_plan_layer(l)
                assert len(plan) == 120
                for j, spec in enumerate(plan):
                    self.slab_list.append(spec + (l * 120 + j, p == 0))
        self.slab_issued = 0
        self.slab_used = 0

    def slab_issue(self):
        i = self.slab_issued
        if i >= len(self.slab_list):
            return
        W2d, r0, nk, c0, nc_, sid, first = self.slab_list[i]
        buf, reg = self.slabs[i % NSLAB]
        dst = buf[:, 0:nk, 0:nc_]
        scr = self.wscr[sid].rearrange("p (k c) -> p k c", c=512)[:, 0:nk, 0:nc_]
        if first:
            src = W2d[r0:r0 + nk * 128, c0:c0 + nc_].rearrange("(k p) c -> p k c", p=128)
            self.S.op("pool", I("dma_start", out=dst, in_=src), writes=[reg], dma=True)
            self.S.op("sp", I("dma_start", out=scr, in_=dst), reads=[reg], writes=[self.wscr_regions[sid]], dma=True)
        else:
            self.S.op("sp", I("dma_start", out=dst, in_=scr), reads=[self.wscr_regions[sid]], writes=[reg], dma=True)
        self.slab_issued += 1

    def slab_next(self, expect):
        i = self.slab_used
        spec = self.slab_list[i]
        assert (spec[1], spec[2], spec[3], spec[4]) == expect[1:], (spec[1:], expect[1:])
        while self.slab_issued < min(i + NSLAB, len(self.slab_list)):
            self.slab_issue()
        self.slab_used += 1
        return self.slabs[i % NSLAB]

    def gemm(self, orient, act, ract, W2d, r0, nkh, c0, ncols, T, nsub, evac):
        S = self.S
        nb = nsub if orient == "tok" else (ncols + 127) // 128
        banks = self.next_gemm_banks(nb)
        for kh in range(nkh):
            slab, rslab = self.slab_next((W2d, r0 + kh * 1024, 8, c0, ncols))
            for b in range(nb):
                bt, br = banks[b]
                for kl in range(8):
                    kc = kh * 8 + kl
                    first = (kh == 0 and kl == 0)
                    last = (kh == nkh - 1 and kl == 7)
                    if orient == "tok":
                        out = bt[:, 0:ncols]
                        lhsT = act(kc)[:, b * 128:(b + 1) * 128]
                        rhs = slab[:, kl, 0:ncols]
                    else:
                        w = min(128, ncols - b * 128)
                        out = bt[0:w, 0:T]
                        lhsT = slab[:, kl, b * 128:b * 128 + w]
                        rhs = act(kc)[:, 0:T]
                    S.op("pe", I("matmul", out, lhsT=lhsT, rhs=rhs, start=first, stop=last),
                         reads=list(ract) + [rslab], writes=[br])
        for b in range(nb):
            bt, br = banks[b]
            evac(b, bt, br)

    def norm_to_hT(self, ps, GT, l):
        S = self.S
        ident = self.CM[:, 0, :]
        nsub = ps.nsub
        for sub in range(nsub):
            S.op("act", I("activation", out=self.HTJ[sub], in_=self.X[:, sub, :], func=AF.Square, accum_out=self.ST[:, sub:sub + 1]),
                 reads=[self.RX], writes=[self.RHT, self.RST])
        self.rstd_op(self.ST[:, 8:8 + nsub], self.ST[:, 0:nsub], D, [self.RST], [self.RST], self.ST[:, 16:16 + nsub])
        for sub in range(nsub):
            xs_ = self.X[:, sub, :]
            rs = self.ST[:, 8 + sub:9 + sub]
            S.op("dve", I("tensor_scalar", out=self.XN[:, :], in0=xs_, scalar1=rs, scalar2=None, op0=ALU.mult),
                 reads=[self.RX, self.RST], writes=[self.RXN])
            for g4 in range(4):
                tv, tr_ = self.next_tv()
                for k4 in range(4):
                    kc = g4 * 4 + k4
                    S.op("pe", I("transpose", out=tv[:, k4 * 128:(k4 + 1) * 128], in_=self.XN[:, kc * 128:(kc + 1) * 128], identity=ident),
                         reads=[self.RXN, self.RCM], writes=[tr_])
                for k4 in range(4):
                    kc = g4 * 4 + k4
                    self.copy_op(self.evac_engine(), self.HT[:, kc, sub * 128:(sub + 1) * 128], tv[:, k4 * 128:(k4 + 1) * 128],
                                 [tr_, self.RG], [self.RHT], scale=GT[:, l, kc:kc + 1])

    def sb_proj(self, ps, l):
        S = self.S
        T, nsub = ps.T, ps.nsub
        act = lambda kc: self.HT[:, kc, 0:T]
        for cg in range(2):
            def evac(j, bt, br, cg=cg):
                h = cg * 4 + j
                self.copy_op(self.evac_engine(), self.QT[:, h, 0:T], bt[:, 0:T], [br], [self.RQT], scale=float(HD) ** -0.5)
            self.gemm("feat", act, [self.RHT], self.w_in[l], 0, 2, C_SBQ + cg * 512, 512, T, nsub, evac)
        for name, c_base, dst_b, rdst in (("k", C_SBK, self.SBK, self.RSBK), ("v", C_SBV, self.SBV, self.RSBV)):
            for cg in range(2):
                def evac(sub, bt, br, cg=cg, name=name, dst_b=dst_b, rdst=rdst):
                    zo, rzo = self.rot(self.ZOUT, "zout")
                    S.op("act", I("activation", out=zo[:, :], in_=bt[:, :], func=AF.Copy), reads=[br], writes=[rzo])
                    S.op("dve", I("tensor_copy", out=dst_b[:, sub, cg * 512:(cg + 1) * 512], in_=zo[:, :]), reads=[rzo], writes=[rdst])
                    ps.store_kv(self, l, name, sub, zo, rzo, cg * 512, 512)
                self.gemm("tok", act, [self.RHT], self.w_in[l], 0, 2, c_base + cg * 512, 512, T, nsub, evac)

    def key_superchunks(self, seg):
        n_own = len(seg.subs)
        tiles = []
        for j in range(n_own - 1, -1, -1):
            c0 = j * 128
            ncol = seg.nq - c0
            tiles.append(dict(own=True, j=j, r=seg.rows, c0=c0, dc=min(128, ncol)))
        nch = seg.hist_n // 512
        chunks = list(range(nch - 1, -1, -1))
        scs = []
        cur = dict(own=tiles, hist=[], order=list(tiles))
        slot = 0
        for t in tiles:
            t["slot"] = slot
            slot += 1
        for c in chunks:
            if slot + 4 > NSLOT:
                scs.append(cur)
                cur = dict(own=[], hist=[], order=[])
                slot = 0
            ch = dict(c=c, slot0=slot, tiles=[])
            for k in range(3, -1, -1):
                t = dict(own=False, kt=c * 4 + k, r=128, c0=0, dc=0, slot=slot + k)
                ch["tiles"].append(t)
                cur["order"].append(t)
            cur["hist"].append(ch)
            slot += 4
        scs.append(cur)
        return scs

    def sb_attention(self, ps, l):
        S = self.S
        ident = self.CM[:, 0, :]
        triGE = self.CM[:, 1, :]
        triLT = self.CM[:, 2, :]
        for seg in ps.segs:
            scs = self.key_superchunks(seg)
            nq = seg.nq
            for hp in range(4):
                state = [self.banks[0], self.banks[1]]
                lsum = [None, None]
                ones = self.CM[:, 5, :]
                first_tile = [True, True]
                n_tiles_total = sum(len(sc["order"]) for sc in scs)
                done = 0
                for sc in scs:
                    for t in sc["own"]:
                        sub = seg.subs[t["j"]]
                        tv, tr_ = self.next_tv()
                        for i in range(2):
                            h = hp * 2 + i
                            S.op("pe", I("transpose", out=tv[:, i * 128:(i + 1) * 128], in_=self.SBK[:, sub, h * 128:(h + 1) * 128], identity=ident),
                                 reads=[self.RSBK, self.RCM], writes=[tr_])
                        for i in range(2):
                            self.copy_op(self.evac_engine(), self.KST[:, i, t["slot"] * 128:(t["slot"] + 1) * 128], tv[:, i * 128:(i + 1) * 128],
                                         [tr_], [self.RKST])
                    for ch in sc["hist"]:
                        c = ch["c"]
                        ktok, rktok = self.rot(self.KTOK, "ktok")
                        srck, rk = seg.hist_src(self, l, "k", c)
                        srcv, rv = seg.hist_src(self, l, "v", c)
                        S.op("pool", I("dma_start", out=ktok[:, :, :], in_=srck[:, hp * 256:(hp + 1) * 256].rearrange("(k p) c -> p k c", p=128)),
                             reads=[rk], writes=[rktok], dma=True)
                        s0 = ch["slot0"]
                        S.op("pool", I("dma_start", out=self.VST[:, s0:s0 + 4, :], in_=srcv[:, hp * 256:(hp + 1) * 256].rearrange("(k p) c -> p k c", p=128)),
                             reads=[rv], writes=[self.RVST], dma=True)
                        for i in range(2):
                            tv, tr_ = self.next_tv()
                            for k in range(4):
                                S.op("pe", I("transpose", out=tv[:, k * 128:(k + 1) * 128], in_=ktok[:, k, i * 128:(i + 1) * 128], identity=ident),
                                     reads=[rktok, self.RCM], writes=[tr_])
                            self.copy_op(self.evac_engine(), self.KST[:, i, s0 * 128:(s0 + 4) * 128], tv[:, :], [tr_], [self.RKST])
                    units = []
                    for t in sc["order"]:
                        done += 1
                        for i in range(2):
                            u = dict(t=t, i=i, h=hp * 2 + i, r=t["r"], c0=t["c0"], ncol=nq - t["c0"],
                                     ft=first_tile[i], last=(done == n_tiles_total))
                            first_tile[i] = False
                            units.append(u)

                    def s0(k):
                        u = units[k]
                        r, c0, ncol = u["r"], u["c0"], u["ncol"]
                        u["z"] = self.banks[4 + self.z_rot % 2]
                        self.z_rot += 1
                        zb, rz = u["z"]
                        ksl = self.KST[:, u["i"], u["t"]["slot"] * 128:u["t"]["slot"] * 128 + r]
                        qsl = self.QT[:, u["h"], seg.qc0 + c0:seg.qc0 + nq]
                        S.op("pe", I("matmul", zb[0:r, 0:ncol], lhsT=ksl, rhs=qsl, start=True, stop=True),
                             reads=[self.RKST, self.RQT], writes=[rz])

                    def s1(k):
                        u = units[k]
                        r, ncol = u["r"], u["ncol"]
                        zb, rz = u["z"]
                        u["E"] = self.rot(self.E, "E")
                        Eb, rE = u["E"]
                        S.op("act", I("activation", out=Eb[0:r, 0:ncol], in_=zb[0:r, 0:ncol], func=AF.Exp), reads=[rz], writes=[rE])
                        if u["t"]["own"]:
                            dc = u["t"]["dc"]
                            S.op("dve", I("tensor_tensor", out=Eb[0:r, 0:dc], in0=Eb[0:r, 0:dc], in1=self.MSB[0:r, 0:dc], op=ALU.mult),
                                 reads=[rE, self.RMSB], writes=[rE])

                    def s2(k):
                        u = units[k]
                        r, ncol = u["r"], u["ncol"]
                        Eb, rE = u["E"]
                        u["L"] = self.rot(self.L, "L")
                        Lb, rL = u["L"]
                        S.op("act", I("activation", out=Lb[0:r, 0:ncol], in_=Eb[0:r, 0:ncol], func=AF.Ln, bias=1.0), reads=[rE], writes=[rL])

                    def s3(k):
                        u = units[k]
                        i, r, c0, ncol, ft = u["i"], u["r"], u["c0"], u["ncol"], u["ft"]
                        Lb, rL = u["L"]
                        u["C"] = self.banks[2 + self.c_rot % 2]
                        self.c_rot += 1
                        cb, rc = u["C"]
                        if not ft:
                            Lp, rLp = lsum[i]
                            S.op("pe", I("matmul", cb[:, 0:ncol], lhsT=ones[:, :], rhs=Lp[:, c0:c0 + ncol], start=True, stop=False),
                                 reads=[rLp, self.RCM], writes=[rc])
                        S.op("pe", I("matmul", cb[:, 0:ncol], lhsT=triGE[0:r, :], rhs=Lb[0:r, 0:ncol], start=ft, stop=True),
                             reads=[rL, self.RCM], writes=[rc])
                        if not u["last"]:
                            self.ls_rot[i] += 1
                            Ln_, rLn = self.LS[i][self.ls_rot[i] % 2]
                            if (c0 > 0) or (r < 128):
                                S.op("pool", I("memset", Ln_[:, 0:nq], 0.0), writes=[rLn])
                            if ft:
                                S.op("dve", I("tensor_copy", out=Ln_[0:r, c0:nq], in_=Lb[0:r, 0:ncol]), reads=[rL], writes=[rLn])
                            else:
                                Lp, rLp = lsum[i]
                                S.op("dve", I("tensor_tensor", out=Ln_[0:r, c0:nq], in0=Lp[0:r, c0:nq], in1=Lb[0:r, 0:ncol], op=ALU.add),
                                     reads=[rLp, rL], writes=[rLn])
                            lsum[i] = (Ln_, rLn)

                    def s4(k):
                        u = units[k]
                        r, ncol = u["r"], u["ncol"]
                        cb, rc = u["C"]
                        u["G"] = self.rot(self.G, "G")
                        Gb, rG = u["G"]
                        S.op("act", I("activation", out=Gb[0:r, 0:ncol], in_=cb[0:r, 0:ncol], func=AF.Exp, scale=-1.0), reads=[rc], writes=[rG])

                    def s5(k):
                        u = units[k]
                        r, ncol = u["r"], u["ncol"]
                        Eb, rE = u["E"]
                        Gb, rG = u["G"]
                        u["A"] = self.rot(self.A, "A")
                        Ab, rA = u["A"]
                        S.op("dve", I("tensor_tensor", out=Ab[0:r, 0:ncol], in0=Eb[0:r, 0:ncol], in1=Gb[0:r, 0:ncol], op=ALU.mult),
                             reads=[rE, rG], writes=[rA])

                    def s6(k):
                        u = units[k]
                        i, h, r, c0, ncol, t = u["i"], u["h"], u["r"], u["c0"], u["ncol"], u["t"]
                        Ab, rA = u["A"]
                        avb, rav = state[i]
                        if t["own"]:
                            sub = seg.subs[t["j"]]
                            vsl = self.SBV[0:r, sub, h * 128:(h + 1) * 128]
                            rvs = self.RSBV
                        else:
                            vsl = self.VST[0:r, t["slot"], i * 128:(i + 1) * 128]
                            rvs = self.RVST
                        S.op("pe", I("matmul", avb[:, c0:c0 + ncol], lhsT=vsl, rhs=Ab[0:r, 0:ncol], start=u["ft"], stop=u["last"], skip_group_check=True),
                             reads=[rvs, rA], writes=[rav])
                    self.pipeline([(0, s0), (1, s1), (2, s2), (3, s3), (4, s4), (5, s5), (6, s6)], len(units))
                for i in range(2):
                    h = hp * 2 + i
                    avb, rav = state[i]
                    self.copy_op(self.evac_engine(), self.SBO[:, h, seg.qc0:seg.qc0 + nq], avb[:, 0:nq], [rav], [self.RSBO])

    def mla_proj(self, ps, l):
        S = self.S
        T, nsub = ps.T, ps.nsub
        ident = self.CM[:, 0, :]
        act = lambda kc: self.HT[:, kc, 0:T]
        for cg in range(4):
            units = []
            self.gemm("tok", act, [self.RHT], self.w_in[l], 0, 2, C_MQ + cg * 384, 384, T, nsub, lambda sub, bt, br: units.append(dict(sub=sub, bt=bt, br=br)))

            def e0(k):
                u = units[k]
                bt, br = u["bt"], u["br"]
                u["stq"] = self.STQ[k % 4]
                stq, rstq = u["stq"]
                for hh in range(2):
                    S.op("act", I("activation", out=self.XN[:, hh * QK:(hh + 1) * QK], in_=bt[:, hh * QK:(hh + 1) * QK], func=AF.Square, accum_out=stq[:, hh:hh + 1]),
                         reads=[br], writes=[self.RXN, rstq])

            def e1(k):
                u = units[k]
                stq, rstq = u["stq"]
                self.rstd_op(stq[:, 2:4], stq[:, 0:2], QK, [rstq], [rstq], stq[:, 4:6])

            def e2(k):
                u = units[k]
                sub, bt, br = u["sub"], u["bt"], u["br"]
                stq, rstq = u["stq"]
                pv = bt[:, 0:384].rearrange("p (h c) -> p h c", c=QK)
                u["qb"] = self.rot(self.QB, "qb")
                qb, rqb = u["qb"]
                rt, rrt = self.rot(self.RT, "rt")
                rt2, rrt2 = self.rot(self.RT2, "rt2")
                cosr = self.ROPE[:, sub, 0:64].rearrange("p (h c) -> p h c", c=32)
                sinr = self.ROPE[:, sub, 64:128].rearrange("p (h c) -> p h c", c=32)
                x1 = pv[:, :, 128:160]
                x2 = pv[:, :, 160:192]
                S.op("dve", I("tensor_tensor", out=rt[:, :, 0:32], in0=x1, in1=cosr, op=ALU.mult), reads=[br, self.RROPE], writes=[rrt])
                S.op("dve", I("tensor_tensor", out=rt2[:, :, 0:32], in0=x2, in1=sinr, op=ALU.mult), reads=[br, self.RROPE], writes=[rrt2])
                S.op("dve", I("tensor_tensor", out=rt[:, :, 32:64], in0=x1, in1=sinr, op=ALU.mult), reads=[br, self.RROPE], writes=[rrt])
                S.op("dve", I("tensor_tensor", out=rt2[:, :, 32:64], in0=x2, in1=cosr, op=ALU.mult), reads=[br, self.RROPE], writes=[rrt2])
                S.op("dve", I("tensor_tensor", out=rt[:, :, 0:32], in0=rt[:, :, 0:32], in1=rt2[:, :, 0:32], op=ALU.subtract), reads=[rrt, rrt2], writes=[rrt])
                S.op("dve", I("tensor_tensor", out=rt[:, :, 32:64], in0=rt[:, :, 32:64], in1=rt2[:, :, 32:64], op=ALU.add), reads=[rrt, rrt2], writes=[rrt])
                for hh in range(2):
                    S.op("dve", I("tensor_scalar", out=qb[:, hh, 0:128], in0=bt[:, hh * QK:hh * QK + 128], scalar1=stq[:, 2 + hh:3 + hh], scalar2=None, op0=ALU.mult),
                         reads=[br, rstq], writes=[rqb])
                    S.op("dve", I("tensor_scalar", out=qb[:, hh, 128:192], in0=rt[:, hh, :], scalar1=stq[:, 2 + hh:3 + hh], scalar2=None, op0=ALU.mult),
                         reads=[rrt, rstq], writes=[rqb])

            def e3(k, cg=cg):
                u = units[k]
                sub = u["sub"]
                qb, rqb = u["qb"]
                tv, tr_ = self.next_tv()
                for hh in range(2):
                    S.op("pe", I("transpose", out=tv[:, hh * 128:(hh + 1) * 128], in_=qb[:, hh, 0:128], identity=ident),
                         reads=[rqb, self.RCM], writes=[tr_])
                    S.op("pe", I("transpose", out=tv[0:64, 256 + hh * 128:256 + (hh + 1) * 128], in_=qb[:, hh, 128:192], identity=ident),
                         reads=[rqb, self.RCM], writes=[tr_])
                for hh in range(2):
                    h = cg * 2 + hh
                    self.copy_op(self.evac_engine(), self.QNT[:, h, sub * 128:(sub + 1) * 128], tv[:, hh * 128:(hh + 1) * 128],
                                 [tr_, self.RCQ], [self.RQNT], scale=self.CQ[:, l, 0:1])
                    self.copy_op(self.evac_engine(), self.QRT[0:64, h, sub * 128:(sub + 1) * 128], tv[0:64, 256 + hh * 128:256 + (hh + 1) * 128],
                                 [tr_, self.RCQ], [self.RQRT], scale=self.CQ[0:64, l, 1:2])
            self.pipeline([(0, e0), (1, e1), (2, e2), (3, e3)], len(units))

        def evac_ckv(sub, bt, br):
            S.op("act", I("activation", out=self.XN[:, 0:LAT], in_=bt[:, :], func=AF.Square, accum_out=self.ST[:, 32:33]),
                 reads=[br], writes=[self.RXN, self.RST])
            self.rstd_op(self.ST[:, 33:34], self.ST[:, 32:33], LAT, [self.RST], [self.RST], self.ST[:, 34:35])
            zo, rzo = self.rot(self.ZOUT, "zout")
            S.op("dve", I("scalar_tensor_tensor", out=zo[:, :], in0=bt[:, :], scalar=self.ST[:, 33:34], in1=self.GKV[:, l, :], op0=ALU.mult, op1=ALU.mult),
                 reads=[br, self.RST, self.RG], writes=[rzo])
            S.op("act", I("activation", out=self.CKVB[:, sub, :], in_=zo[:, :], func=AF.Copy), reads=[rzo], writes=[self.RCKVB])
            ps.store_kv(self, l, "ckv", sub, zo, rzo, 0, LAT)
        self.gemm("tok", act, [self.RHT], self.w_in[l], 0, 2, C_CKV, 512, T, nsub, evac_ckv)

        def evac_kr(sub, bt, br):
            zo, rzo = self.rot(self.ZOUT, "zout")
            rt, rrt = self.rot(self.RT, "rt")
            rt2, rrt2 = self.rot(self.RT2, "rt2")
            cos = self.ROPE[:, sub, 0:32]
            sin = self.ROPE[:, sub, 64:96]
            x1 = bt[:, 0:32]
            x2 = bt[:, 32:64]
            S.op("dve", I("tensor_tensor", out=rt[:, 0, 0:32], in0=x1, in1=cos, op=ALU.mult), reads=[br, self.RROPE], writes=[rrt])
            S.op("dve", I("tensor_tensor", out=rt2[:, 0, 0:32], in0=x2, in1=sin, op=ALU.mult), reads=[br, self.RROPE], writes=[rrt2])
            S.op("dve", I("tensor_tensor", out=rt[:, 0, 32:64], in0=x1, in1=sin, op=ALU.mult), reads=[br, self.RROPE], writes=[rrt])
            S.op("dve", I("tensor_tensor", out=rt2[:, 0, 32:64], in0=x2, in1=cos, op=ALU.mult), reads=[br, self.RROPE], writes=[rrt2])
            S.op("dve", I("tensor_tensor", out=zo[:, 0:32], in0=rt[:, 0, 0:32], in1=rt2[:, 0, 0:32], op=ALU.subtract), reads=[rrt, rrt2], writes=[rzo])
            S.op("dve", I("tensor_tensor", out=zo[:, 32:64], in0=rt[:, 0, 32:64], in1=rt2[:, 0, 32:64], op=ALU.add), reads=[rrt, rrt2], writes=[rzo])
            S.op("act", I("activation", out=self.KRB[:, sub, :], in_=zo[:, 0:64], func=AF.Copy), reads=[rzo], writes=[self.RKRB])
            ps.store_kv(self, l, "kr", sub, zo, rzo, 0, ROPE)
        self.gemm("tok", act, [self.RHT], self.w_in[l], 0, 2, C_KR, 64, T, nsub, evac_kr)

    def mla_attention(self, ps, l):
        S = self.S
        ident = self.CM[:, 0, :]
        ones = self.CM[:, 5, :]
        for seg in ps.segs:
            scs = self.key_superchunks(seg)
            nq = seg.nq
            for hp in range(4):
                wuk, rwuk = self.rot(self.WUK, "wuk")
                wuv, rwuv = self.rot(self.WUV, "wuv")
                S.op("pool", I("dma_start", out=wuk[:, :, :], in_=self.w_uk[l][:, hp * 256:(hp + 1) * 256].rearrange("(k p) c -> p k c", p=128)),
                     writes=[rwuk], dma=True)
                S.op("pool", I("dma_start", out=wuv[:, :, :], in_=self.w_uv[l][:, hp * 256:(hp + 1) * 256].rearrange("(k p) c -> p k c", p=128)),
                     writes=[rwuv], dma=True)
                state = [(self.banks[0], self.banks[1]), (self.banks[2], self.banks[3])]
                first_tile = [True, True]
                n_tiles_total = sum(len(sc["order"]) for sc in scs)
                done = 0
                for sc in scs:
                    groups = []
                    for t in sc["own"]:
                        sub = seg.subs[t["j"]]
                        s = t["slot"]
                        tv, tr_ = self.next_tv()
                        for kc in range(4):
                            S.op("pe", I("transpose", out=tv[:, kc * 128:(kc + 1) * 128], in_=self.CKVB[:, sub, kc * 128:(kc + 1) * 128], identity=ident),
                                 reads=[self.RCKVB, self.RCM], writes=[tr_])
                        for kc in range(4):
                            self.copy_op(self.evac_engine(), self.CKVT[:, kc, s * 128:(s + 1) * 128], tv[:, kc * 128:(kc + 1) * 128], [tr_], [self.RCKVT])
                        tv2, tr2_ = self.next_tv()
                        S.op("pe", I("transpose", out=tv2[0:64, 0:128], in_=self.KRB[:, sub, :], identity=ident),
                             reads=[self.RKRB, self.RCM], writes=[tr2_])
                        self.copy_op(self.evac_engine(), self.KRT[0:64, s * 128:(s + 1) * 128], tv2[0:64, 0:128], [tr2_], [self.RKRT])
                        S.op("act", I("activation", out=self.XN[:, 0:64], in_=self.KRB[:, sub, :], func=AF.Square, accum_out=self.SSR[:, s:s + 1]),
                             reads=[self.RKRB], writes=[self.RXN, self.RSSR])
                    if sc["own"]:
                        groups.append((0, len(sc["own"])))
                    for ch in sc["hist"]:
                        c = ch["c"]
                        s0 = ch["slot0"]
                        ctok, rctok = self.rot(self.CTOK, "ctok")
                        krtok, rkrtok = self.rot(self.KRTOK, "krtok")
                        srcc, rc_ = seg.hist_src(self, l, "ckv", c)
                        srcr, rr_ = seg.hist_src(self, l, "kr", c)
                        S.op("pool", I("dma_start", out=ctok[:, :, :], in_=srcc.rearrange("(k p) c -> p k c", p=128)),
                             reads=[rc_], writes=[rctok], dma=True)
                        S.op("pool", I("dma_start", out=krtok[:, :, :], in_=srcr.rearrange("(k p) c -> p k c", p=128)),
                             reads=[rr_], writes=[rkrtok], dma=True)
                        for kc in range(4):
                            tv, tr_ = self.next_tv()
                            for k in range(4):
                                S.op("pe", I("transpose", out=tv[:, k * 128:(k + 1) * 128], in_=ctok[:, k, kc * 128:(kc + 1) * 128], identity=ident),
                                     reads=[rctok, self.RCM], writes=[tr_])
                            self.copy_op(self.evac_engine(), self.CKVT[:, kc, s0 * 128:(s0 + 4) * 128], tv[:, :], [tr_], [self.RCKVT])
                        tv2, tr2_ = self.next_tv()
                        for k in range(4):
                            S.op("pe", I("transpose", out=tv2[0:64, k * 128:(k + 1) * 128], in_=krtok[:, k, :], identity=ident),
                                 reads=[rkrtok, self.RCM], writes=[tr2_])
                        self.copy_op(self.evac_engine(), self.KRT[0:64, s0 * 128:(s0 + 4) * 128], tv2[0:64, :], [tr2_], [self.RKRT])
                        for k in range(4):
                            S.op("act", I("activation", out=self.XN[:, 0:64], in_=krtok[:, k, :], func=AF.Square, accum_out=self.SSR[:, s0 + k:s0 + k + 1]),
                                 reads=[rkrtok], writes=[self.RXN, self.RSSR])
                        groups.append((s0, 4))
                    stb, rstb = self.STB
                    for (s0, nt) in groups:
                        ncols = nt * 128
                        for i in range(2):
                            zb, rz = self.banks[4 + self.z_rot % 2]
                            self.z_rot += 1
                            for kc in range(4):
                                S.op("pe", I("matmul", zb[:, 0:ncols], lhsT=wuk[:, kc, i * 128:(i + 1) * 128], rhs=self.CKVT[:, kc, s0 * 128:s0 * 128 + ncols], start=(kc == 0), stop=(kc == 3)),
                                     reads=[rwuk, self.RCKVT], writes=[rz])
                            S.op("dve", I("tensor_copy", out=self.KST[:, i, s0 * 128:s0 * 128 + ncols], in_=zb[:, 0:ncols]),
                                 reads=[rz], writes=[self.RKST])
                            sq, rsq = self.rot(self.SQ, "sq")
                            S.op("act", I("activation", out=sq[:, 0:ncols], in_=zb[:, 0:ncols], func=AF.Square),
                                 reads=[rz], writes=[rsq])
                            for k in range(nt):
                                col = i * NSLOT + s0 + k
                                S.op("pe", I("matmul", stb[:, col:col + 1], lhsT=sq[:, k * 128:(k + 1) * 128], rhs=ones[:, 0:1], start=True, stop=True),
                                     reads=[rsq, self.RCM], writes=[rstb])
                        for k in range(nt):
                            zb, rz = self.banks[4 + self.z_rot % 2]
                            self.z_rot += 1
                            for kc in range(4):
                                S.op("pe", I("matmul", zb[:, 0:256], lhsT=self.CKVT[:, kc, (s0 + k) * 128:(s0 + k + 1) * 128], rhs=wuv[:, kc, :], start=(kc == 0), stop=(kc == 3)),
                                     reads=[rwuv, self.RCKVT], writes=[rz])
                            self.copy_op(self.evac_engine(), self.VST[:, s0 + k, :], zb[:, 0:256], [rz], [self.RVST])
                    nslots_used = max(s0 + nt for (s0, nt) in groups)
                    for i in range(2):
                        S.op("dve", I("tensor_tensor", out=self.RSTDK[:, i, 0:nslots_used], in0=stb[:, i * NSLOT:i * NSLOT + nslots_used], in1=self.SSR[:, 0:nslots_used], op=ALU.add),
                             reads=[rstb, self.RSSR], writes=[self.RRSTDK])
                    rk_all = self.RSTDK[:, :, 0:nslots_used]
                    S.op("dve", I("tensor_scalar", out=rk_all, in0=rk_all, scalar1=1.0 / QK, scalar2=EPS, op0=ALU.mult, op1=ALU.add),
                         reads=[self.RRSTDK], writes=[self.RRSTDK])
                    S.op("act", I("activation", out=rk_all, in_=rk_all, func=AF.Ln), reads=[self.RRSTDK], writes=[self.RRSTDK])
                    S.op("act", I("activation", out=rk_all, in_=rk_all, func=AF.Exp, scale=-0.5), reads=[self.RRSTDK], writes=[self.RRSTDK])
                    units = []
                    for t in sc["order"]:
                        done += 1
                        for i in range(2):
                            u = dict(t=t, i=i, h=hp * 2 + i, r=t["r"], c0=t["c0"], ncol=nq - t["c0"], s=t["slot"],
                                     ft=first_tile[i], last=(done == n_tiles_total))
                            first_tile[i] = False
                            units.append(u)

                    def m0(k):
                        u = units[k]
                        i, h, r, c0, ncol, sl = u["i"], u["h"], u["r"], u["c0"], u["ncol"], u["s"]
                        u["z"] = self.banks[4 + self.z_rot % 2]
                        self.z_rot += 1
                        zb, rz = u["z"]
                        S.op("pe", I("matmul", zb[0:r, 0:ncol], lhsT=self.KST[:, i, sl * 128:sl * 128 + r], rhs=self.QNT[:, h, seg.qc0 + c0:seg.qc0 + nq], start=True, stop=False),
                             reads=[self.RKST, self.RQNT], writes=[rz])
                        S.op("pe", I("matmul", zb[0:r, 0:ncol], lhsT=self.KRT[0:64, sl * 128:sl * 128 + r], rhs=self.QRT[0:64, h, seg.qc0 + c0:seg.qc0 + nq], start=False, stop=True),
                             reads=[self.RKRT, self.RQRT], writes=[rz])

                    def m1(k):
                        u = units[k]
                        i, r, ncol, sl = u["i"], u["r"], u["ncol"], u["s"]
                        zb, rz = u["z"]
                        u["P"] = self.rot(self.P, "P")
                        Pb, rP = u["P"]
                        S.op("act", I("activation", out=Pb[0:r, 0:ncol], in_=zb[0:r, 0:ncol], func=AF.Exp, scale=self.RSTDK[0:r, i, sl:sl + 1]),
                             reads=[rz, self.RRSTDK], writes=[rP])
                        if u["t"]["own"]:
                            dc = u["t"]["dc"]
                            S.op("dve", I("tensor_tensor", out=Pb[0:r, 0:dc], in0=Pb[0:r, 0:dc], in1=self.CM[0:r, 4, 0:dc], op=ALU.mult),
                                 reads=[rP, self.RCM], writes=[rP])

                    def m2(k):
                        u = units[k]
                        i, r, c0, ncol, sl = u["i"], u["r"], u["c0"], u["ncol"], u["s"]
                        Pb, rP = u["P"]
                        (sumb, rsum), (avb, rav) = state[i]
                        S.op("pe", I("matmul", avb[:, c0:c0 + ncol], lhsT=self.VST[0:r, sl, i * 128:(i + 1) * 128], rhs=Pb[0:r, 0:ncol], start=u["ft"], stop=u["last"], skip_group_check=True),
                             reads=[self.RVST, rP], writes=[rav])
                        S.op("pe", I("matmul", sumb[:, c0:c0 + ncol], lhsT=ones[0:r, :], rhs=Pb[0:r, 0:ncol], start=u["ft"], stop=u["last"], skip_group_check=True),
                             reads=[self.RCM, rP], writes=[rsum])
                    self.pipeline([(0, m0), (1, m1), (2, m2)], len(units))
                for i in range(2):
                    h = hp * 2 + i
                    (sumb, rsum), (avb, rav) = state[i]
                    Rb, rR = self.rot(self.R, "R")
                    S.op("dve", I("reciprocal", out=Rb[:, 0:nq], in_=sumb[:, 0:nq]), reads=[rsum], writes=[rR])
                    S.op("dve", I("tensor_tensor", out=self.MLO[:, h, seg.qc0:seg.qc0 + nq], in0=avb[:, 0:nq], in1=Rb[:, 0:nq], op=ALU.mult),
                         reads=[rav, rR], writes=[self.RMLO])

    def merge(self, ps, l):
        S = self.S
        T, nsub = ps.T, ps.nsub
        hact = lambda kc: self.HT[:, kc, 0:T]
        for cg in range(4):
            def evac_ymla(j, bt, br):
                S.op("act", I("activation", out=self.TMPA[:, j, 0:T], in_=bt[:, 0:T], func=AF.Copy), reads=[br], writes=[self.RTMPA])
            self.gemm("feat", lambda kc: self.MLO[:, kc, 0:T], [self.RMLO], self.w_mla_proj[l], 0, 1, cg * 512, 512, T, nsub, evac_ymla)

            def evac_gmla(j, bt, br):
                sg, rsg = self.rot(self.SG, "sg")
                S.op("act", I("activation", out=sg[:, 0:T], in_=bt[:, 0:T], func=AF.Sigmoid), reads=[br], writes=[rsg])
                S.op("dve", I("tensor_tensor", out=self.TMPA[:, j, 0:T], in0=self.TMPA[:, j, 0:T], in1=sg[:, 0:T], op=ALU.mult),
                     reads=[rsg, self.RTMPA], writes=[self.RTMPA])
            self.gemm("feat", hact, [self.RHT], self.w_in[l], 0, 2, C_GMLA + cg * 512, 512, T, nsub, evac_gmla)
            def evac_ysb(j, bt, br):
                S.op("act", I("activation", out=self.TMPB[:, j, 0:T], in_=bt[:, 0:T], func=AF.Copy), reads=[br], writes=[self.RTMPB])
            self.gemm("feat", lambda kc: self.SBO[:, kc, 0:T], [self.RSBO], self.w_sb_proj[l], 0, 1, cg * 512, 512, T, nsub, evac_ysb)

            def evac_gsb(j, bt, br, cg=cg):
                sg, rsg = self.rot(self.SG, "sg")
                S.op("act", I("activation", out=sg[:, 0:T], in_=bt[:, 0:T], func=AF.Sigmoid), reads=[br], writes=[rsg])
                S.op("dve", I("tensor_tensor", out=sg[:, 0:T], in0=self.TMPB[:, j, 0:T], in1=sg[:, 0:T], op=ALU.mult), reads=[self.RTMPB, rsg], writes=[rsg])
                S.op("dve", I("tensor_tensor", out=self.UT[:, cg * 4 + j, 0:T], in0=sg[:, 0:T], in1=self.TMPA[:, j, 0:T], op=ALU.add),
                     reads=[rsg, self.RTMPA], writes=[self.RUT])
            self.gemm("feat", hact, [self.RHT], self.w_in[l], 0, 2, C_GSB + cg * 512, 512, T, nsub, evac_gsb)
        for cg in range(4):
            def evac_o(sub, bt, br, cg=cg):
                xs_ = self.X[:, sub, cg * 512:(cg + 1) * 512]
                S.op("dve", I("tensor_tensor", out=xs_, in0=bt[:, :], in1=xs_, op=ALU.add), reads=[br, self.RX], writes=[self.RX])
            self.gemm("tok", lambda kc: self.UT[:, kc, 0:T], [self.RUT], self.w_o[l], 0, 2, cg * 512, 512, T, nsub, evac_o)

    def ffn(self, ps, l):
        S = self.S
        T, nsub = ps.T, ps.nsub
        hact = lambda kc: self.HT[:, kc, 0:T]
        for q in range(4):
            aT, raT = self.AT[q % 2]
            for cg in range(4):
                def evac_up(j, bt, br, cg=cg):
                    Eb, rE = self.rot(self.E, "E")
                    S.op("act", I("activation", out=Eb[:, 0:T], in_=bt[:, 0:T], func=AF.Relu), reads=[br], writes=[rE])
                    S.op("dve", I("tensor_tensor", out=aT[:, cg * 4 + j, 0:T], in0=Eb[:, 0:T], in1=Eb[:, 0:T], op=ALU.mult), reads=[rE], writes=[raT])
                self.gemm("feat", hact, [self.RHT], self.w_up[l], 0, 2, q * 2048 + cg * 512, 512, T, nsub, evac_up)
            for cg in range(4):
                def evac_dn(sub, bt, br, cg=cg):
                    xs_ = self.X[:, sub, cg * 512:(cg + 1) * 512]
                    S.op("dve", I("tensor_tensor", out=xs_, in0=bt[:, :], in1=xs_, op=ALU.add), reads=[br, self.RX], writes=[self.RX])
                self.gemm("tok", lambda kc: aT[:, kc, 0:T], [raT], self.w_down[l], q * 2048, 2, cg * 512, 512, T, nsub, evac_dn)

    def load_consts(self):
        S = self.S
        S.op("pool", I("dma_start", out=self.CM[:, :, :], in_=self.c_mats.rearrange("m p c -> p m c")), writes=[self.RCM], dma=True)
        S.op("sp", I("dma_start", out=self.MSB[:, :], in_=self.c_mats[3]), writes=[self.RMSB], dma=True)
        for l in range(DEPTH):
            S.op("sp", I("dma_start", out=self.G1T[:, l, :], in_=self.norm1_g[l].rearrange("(k p) -> p k", p=128), allow_slow_non_contiguous=True), writes=[self.RG], dma=True)
            S.op("sp", I("dma_start", out=self.G2T[:, l, :], in_=self.norm2_g[l].rearrange("(k p) -> p k", p=128), allow_slow_non_contiguous=True), writes=[self.RG], dma=True)
            S.op("sp", I("dma_start", out=self.GKV[:, l, :], in_=self.kv_norm_g[l:l + 1, :].to_broadcast([128, LAT])), writes=[self.RG], dma=True)
            for ci, (src, a, b) in enumerate(((self.q_norm_g, 0, 128), (self.k_norm_g, 0, 128), (self.q_norm_g, 128, 192), (self.k_norm_g, 128, 192))):
                S.op("sp", I("dma_start", out=self.GQK[0:b - a, l, ci:ci + 1], in_=src[l, a:b].rearrange("(p o) -> p o", o=1), allow_slow_non_contiguous=True),
                     writes=[self.RG], dma=True)
            sc = float(QK) ** -0.5
            S.op("dve", I("scalar_tensor_tensor", out=self.CQ[:, l, 0:1], in0=self.GQK[:, l, 0:1], scalar=sc, in1=self.GQK[:, l, 1:2], op0=ALU.mult, op1=ALU.mult),
                 reads=[self.RG], writes=[self.RCQ])
            S.op("dve", I("scalar_tensor_tensor", out=self.CQ[0:64, l, 1:2], in0=self.GQK[0:64, l, 2:3], scalar=sc, in1=self.GQK[0:64, l, 3:4], op0=ALU.mult, op1=ALU.mult),
                 reads=[self.RG], writes=[self.RCQ])

    def run_pass(self, ps):
        S = self.S
        ps.load_x(self)
        for l in range(self.n_layers):
            self.norm_to_hT(ps, self.G1T, l)
            self.phase_switch([self.RQT, self.RSBK, self.RSBV], self.arena_regions)
            self.phase_switch([self.RKST, self.RVST], self.staging_regions)
            self.phase_switch(self.ls_regions, [])
            self.sb_proj(ps, l)
            self.phase_switch([self.G[0][1], self.G[1][1]], [self.RXN])
            self.sb_attention(ps, l)
            self.phase_switch([self.RXN], [self.G[0][1], self.G[1][1]])
            self.phase_switch([self.RQNT, self.RQRT, self.RCKVB, self.RKRB, self.WUK[0][1], self.WUK[1][1], self.WUV[0][1], self.WUV[1][1]], self.arena_regions)
            self.phase_switch(self.ls_regions, [])
            self.mla_proj(ps, l)
            self.mla_attention(ps, l)
            self.phase_switch([self.RUT], self.arena_regions)
            self.phase_switch([self.RTMPA, self.RTMPB, self.SG[0][1], self.SG[1][1]], self.staging_regions)
            self.merge(ps, l)
            self.norm_to_hT(ps, self.G2T, l)
            self.phase_switch([self.AT[0][1], self.AT[1][1]], self.staging_regions)
            self.ffn(ps, l)
        ps.store_y(self)

    def build(self):
        self.alloc()
        passes = [make_pass(self, cfg) for cfg in self.passes_cfg]
        self.slab_init(len(passes))
        self.load_consts()
        for ps in passes:
            self.run_pass(ps)
        self.S.finish()
        self.S.emit()
        return self.nc


def hist_region(B, key):
    r = B.hist_regions.get(key)
    if r is None:
        r = Region("hist" + str(key))
        B.hist_regions[key] = r
    return r


def make_pass(B, cfg):
    ps = Pass()
    kind = cfg[0]
    ps.kind = kind
    if kind == "prompt":
        _, pb, tb = cfg
        ps.T, ps.nsub = 512, 4
        seg = Seg()
        seg.subs = [0, 1, 2, 3]
        seg.rows = 128
        seg.nq = 512
        seg.qc0 = 0
        seg.hist_n = 512 * tb
        outs = {"k": B.pk, "v": B.pv, "ckv": B.pckv, "kr": B.pkr}

        def hist_src(B_, l, name, c):
            return outs[name][l, pb, c * 512:(c + 1) * 512, :], hist_region(B_, (name, l, pb, c))
        seg.hist_src = hist_src
        ps.segs = [seg]

        def load_x(B_):
            src = B_.xp[pb, tb * 512:(tb + 1) * 512, :].rearrange("(s p) d -> p s d", p=128)
            for s in range(4):
                B_.S.op("sp", I("dma_start", out=B_.X[:, s, :], in_=B_.xp[pb, tb * 512 + s * 128:tb * 512 + (s + 1) * 128, :]), writes=[B_.RX], dma=True)
            B_.S.op("sp", I("dma_start", out=B_.ROPE[:, :, :], in_=B_.rope_p[tb * 512:(tb + 1) * 512, :].rearrange("(s p) c -> p s c", p=128)), writes=[B_.RROPE], dma=True)
        ps.load_x = load_x

        def store_y(B_):
            for s in range(4):
                B_.S.op("sp", I("dma_start", out=B_.yp[pb, tb * 512 + s * 128:tb * 512 + (s + 1) * 128, :], in_=B_.X[:, s, :]), reads=[B_.RX], cowrites=[B_.Rdram_out], dma=True, is_store=True)
        ps.store_y = store_y

        def store_kv(B_, l, name, sub, zo, rzo, c0, ncols):
            if name == "kr":
                B_.store_kr_rows(zo, rzo, 128, outs[name][l, pb, tb * 512 + sub * 128:tb * 512 + (sub + 1) * 128, :],
                                 hist_region(B_, (name, l, pb, tb)), True)
                return
            dst = outs[name][l, pb, tb * 512 + sub * 128:tb * 512 + (sub + 1) * 128, c0:c0 + ncols]
            B_.S.op("sp", I("dma_start", out=dst, in_=zo[:, 0:ncols]), reads=[rzo], cowrites=[hist_region(B_, (name, l, pb, tb))], dma=True, is_store=True)
        ps.store_kv = store_kv
    else:
        ps.T, ps.nsub = 256, 2
        caches = {"k": B.csk, "v": B.csv, "ckv": B.cckv, "kr": B.ckr}
        outs = {"k": B.sk, "v": B.sv, "ckv": B.sckv, "kr": B.skr}
        Rin = Region("cache_inputs")
        ps.segs = []
        for s in range(2):
            seg = Seg()
            seg.subs = [s]
            seg.rows = 64
            seg.nq = 64
            seg.qc0 = s * 128
            seg.hist_n = PAST

            def hist_src(B_, l, name, c, s=s):
                return caches[name][l, s, c * 512:(c + 1) * 512, :], Rin
            seg.hist_src = hist_src
            ps.segs.append(seg)

        def load_x(B_):
            B_.S.op("dve", I("memset", B_.X[:, 0:2, :], 0.0), writes=[B_.RX])
            B_.S.op("dve", I("memset", B_.SBO[:, :, 0:256], 0.0), writes=[B_.RSBO])
            B_.S.op("dve", I("memset", B_.MLO[:, :, 0:256], 0.0), writes=[B_.RMLO])
            for s in range(2):
                B_.S.op("sp", I("dma_start", out=B_.X[0:64, s, :], in_=B_.xs[s, :, :]), writes=[B_.RX], dma=True)
                B_.S.op("sp", I("dma_start", out=B_.ROPE[:, s, :], in_=B_.rope_s[:, :]), writes=[B_.RROPE], dma=True)
        ps.load_x = load_x

        def store_y(B_):
            for s in range(2):
                B_.S.op("sp", I("dma_start", out=B_.ys[s, :, :], in_=B_.X[0:64, s, :]), reads=[B_.RX], cowrites=[B_.Rdram_out], dma=True, is_store=True)
        ps.store_y = store_y

        def store_kv(B_, l, name, sub, zo, rzo, c0, ncols):
            if name == "kr":
                B_.store_kr_rows(zo, rzo, DSEQ, outs[name][l, sub, :, :], B_.Rdram_out, True)
                return
            dst = outs[name][l, sub, :, c0:c0 + ncols]
            B_.S.op("sp", I("dma_start", out=dst, in_=zo[0:64, 0:ncols]), reads=[rzo], cowrites=[B_.Rdram_out], dma=True, is_store=True)
        ps.store_kv = store_kv
    return ps


def const_inputs():
    k = np.arange(128)[:, None]
    q = np.arange(128)[None, :]
    ident = np.eye(128, dtype=np.float32)
    trige = (k >= q).astype(np.float32)
    trilt = (k < q).astype(np.float32)
    mask_sb = (k < q).astype(np.float32)
    mask_mla = ((k // 64) <= (q // 64)).astype(np.float32)
    ones = np.ones((128, 128), np.float32)
    c_mats = np.stack([ident, trige, trilt, mask_sb, mask_mla, ones]).astype(np.float32)
    half = ROPE // 2
    inv_freq = (np.float32(10000.0) ** (-np.arange(half, dtype=np.float32) / np.float32(half))).astype(np.float32)

    def table(pos):
        ang = pos.astype(np.float32)[:, None] * inv_freq[None, :]
        c = np.cos(ang).astype(np.float32)
        s = np.sin(ang).astype(np.float32)
        return np.concatenate([c, c, s, s], axis=1).astype(np.float32)
    rope_p = table(np.arange(SEQ))
    rope_s = np.zeros((128, 128), np.float32)
    rope_s[:DSEQ] = table(PAST + np.arange(DSEQ))
    return c_mats, rope_p, rope_s


ALL_PASSES = [("prompt", pb, tb) for pb in range(2) for tb in range(4)] + [("sample",)]
_prog_cache = {}


def get_program(passes_cfg, n_layers=DEPTH):
    key = (tuple(passes_cfg), n_layers)
    if key not in _prog_cache:
        global _last_builder
        _last_builder = Builder(list(passes_cfg), n_layers)
        _prog_cache[key] = _last_builder.build()
    return _prog_cache[key]


def run(inputs, cores=None, passes_cfg=None, n_layers=DEPTH):
    cores = list(range(NCORES)) if cores is None else cores
    passes_cfg = ALL_PASSES if passes_cfg is None else passes_cfg
    nc = get_program(passes_cfg, n_layers)
    c_mats, rope_p, rope_s = const_inputs()
    f = lambda a: np.ascontiguousarray(np.asarray(a, dtype=np.float32))
    shared = {k: f(inputs[k]) for k in ("norm1_g", "norm2_g", "q_norm_g", "k_norm_g", "kv_norm_g", "w_in", "w_uk", "w_uv",
                                        "w_sb_proj", "w_mla_proj", "w_o", "w_up", "w_down")}
    shared.update(c_mats=c_mats, rope_p=rope_p, rope_s=rope_s)
    in_maps = []
    for c in cores:
        m = dict(shared)
        b0 = 2 * c
        m["xp"] = f(inputs["x_prompt"][b0:b0 + 2])
        m["xs"] = f(inputs["x_sample"][b0:b0 + 2])
        m["csk"] = f(np.asarray(inputs["cache_sb_k"])[:, b0:b0 + 2].reshape(DEPTH, 2, PAST, 1024))
        m["csv"] = f(np.asarray(inputs["cache_sb_v"])[:, b0:b0 + 2].reshape(DEPTH, 2, PAST, 1024))
        m["cckv"] = f(np.asarray(inputs["cache_mla_ckv"])[:, b0:b0 + 2])
        m["ckr"] = f(np.asarray(inputs["cache_mla_krope"])[:, b0:b0 + 2])
        in_maps.append(m)
    res = run_bass_kernel_spmd(nc, in_maps, core_ids=list(range(len(cores))))
    return res.results


def kernel(**inputs):
    results = run(inputs)
    B = 2 * NCORES
    cat1 = lambda name: np.concatenate([r[name] for r in results], axis=0)
    cat2 = lambda name: np.concatenate([r[name] for r in results], axis=1)
    y_prompt = cat1("yp").astype(np.float32)
    y_sample = cat1("ys").astype(np.float32)
    p_k = cat2("pk").reshape(DEPTH, B, SEQ, NH, HD).astype(np.float32)
    p_v = cat2("pv").reshape(DEPTH, B, SEQ, NH, HD).astype(np.float32)
    p_ckv = cat2("pckv").astype(np.float32)
    p_kr = cat2("pkr").astype(np.float32)
    s_k = cat2("sk").reshape(DEPTH, B, DSEQ, NH, HD).astype(np.float32)
    s_v = cat2("sv").reshape(DEPTH, B, DSEQ, NH, HD).astype(np.float32)
    s_ckv = cat2("sckv").astype(np.float32)
    s_kr = cat2("skr").astype(np.float32)
    return (y_prompt, y_sample, p_k, p_v, p_ckv, p_kr, s_k, s_v, s_ckv, s_kr)
```
